# Optimizing a Trainium2 kernel written in Bass

```python
import math
import jax, jax.numpy as jnp
from jax import lax
import numpy as np

D_MODEL = 4096
BATCH = 4
SEQ = 4096
DEPTH = 1

D_HEAD = 128
D_ATTN = D_MODEL // 2
N_ATTN_HEADS = D_ATTN // D_HEAD
D_RNN = D_MODEL - D_ATTN
N_RNN_BLOCKS = 16
RNN_BLOCK = D_RNN // N_RNN_BLOCKS
D_MIX = D_ATTN + D_RNN
D_IN = 3 * D_ATTN + 2 * D_RNN
CONV_WIDTH = 4
RGLRU_C = 8.0
D_FF = int(math.ceil(8 * D_MODEL / (3 * 256))) * 256
PLE_DIM = 256
SB_BLOCK = 128
NORM_EPS = 1e-6

kernel_name = "hymba_style_stickbreak_rglru_layer"


def rms_norm(x, g):
    xf = x.astype(jnp.float32)
    y = xf * lax.rsqrt(jnp.mean(xf * xf, axis=-1, keepdims=True) + NORM_EPS)
    return (y * g.astype(jnp.float32)).astype(x.dtype)


def stick_breaking_attention(q, k, v):
    S = q.shape[1]
    dh = q.shape[-1]
    scale = dh ** -0.5
    outs = []
    for blk in range(S // SB_BLOCK):
        q0 = blk * SB_BLOCK
        kv_len = q0 + SB_BLOCK
        qb = q[:, q0:kv_len]
        kb = k[:, :kv_len]
        vb = v[:, :kv_len]
        z = jnp.einsum("bqhd,bkhd->bhqk", qb, kb).astype(jnp.float32) * scale
        t_idx = q0 + jnp.arange(SB_BLOCK)[:, None]
        s_idx = jnp.arange(kv_len)[None, :]
        causal = s_idx < t_idx
        log_keep = jnp.where(causal, -jax.nn.softplus(z), 0.0)
        after = lax.cumsum(log_keep, axis=3, reverse=True) - log_keep
        w = jnp.where(causal, jnp.exp(jax.nn.log_sigmoid(z) + after), 0.0)
        outs.append(jnp.einsum("bhqk,bkhd->bqhd", w.astype(vb.dtype), vb))
    return jnp.concatenate(outs, axis=1)


def causal_depthwise_conv(x, w, b):
    c = x.shape[-1]
    y = lax.conv_general_dilated(
        x, w[:, None, :].astype(x.dtype), window_strides=(1,),
        padding=[(CONV_WIDTH - 1, 0)],
        dimension_numbers=("NWC", "WIO", "NWC"),
        feature_group_count=c)
    return y + b.astype(x.dtype)


def rg_lru(x, w_a, b_a, w_x, b_x, lam):
    B, S, C = x.shape
    xb = x.reshape(B, S, N_RNN_BLOCKS, RNN_BLOCK)
    r = jax.nn.sigmoid(jnp.einsum("bsni,nij->bsnj", xb, w_a).reshape(B, S, C).astype(jnp.float32)
                       + b_a.astype(jnp.float32))
    i = jax.nn.sigmoid(jnp.einsum("bsni,nij->bsnj", xb, w_x).reshape(B, S, C).astype(jnp.float32)
                       + b_x.astype(jnp.float32))
    log_a = -RGLRU_C * r * jax.nn.softplus(-lam.astype(jnp.float32))
    a = jnp.exp(log_a)
    mult = jnp.sqrt(-jnp.expm1(2.0 * log_a))
    u = mult * (i * x.astype(jnp.float32))

    def combine(left, right):
        a1, b1 = left
        a2, b2 = right
        return a1 * a2, a2 * b1 + b2

    _, h = lax.associative_scan(combine, (a, u), axis=1)
    return h.astype(x.dtype)


def setup_inputs(seed: int = 0) -> dict:
    key = jax.random.key(seed)
    ks = jax.random.split(key, 24)
    f32 = jnp.float32

    def nrm(k, shape, fan_in):
        return jax.random.normal(k, shape, f32) * (fan_in ** -0.5)

    def gain(k, shape):
        return 1.0 + 0.02 * jax.random.normal(k, shape, f32)

    def bias(k, shape):
        return 0.02 * jax.random.normal(k, shape, f32)

    x = jax.random.normal(ks[0], (BATCH, SEQ, D_MODEL), f32)
    p = jax.random.normal(ks[1], (DEPTH, BATCH, SEQ, PLE_DIM), f32)
    a0 = jax.random.uniform(ks[9], (DEPTH, D_RNN), f32, 0.9, 0.999)
    base = a0 ** (1.0 / RGLRU_C)
    rg_lambda = jnp.log(base) - jnp.log1p(-base)
    return {
        "x": x,
        "p": p,
        "g_mix": gain(ks[2], (DEPTH, D_MODEL)),
        "w_in": nrm(ks[3], (DEPTH, D_MODEL, D_IN), D_MODEL),
        "conv_w": nrm(ks[4], (DEPTH, CONV_WIDTH, D_RNN), CONV_WIDTH),
        "conv_b": bias(ks[5], (DEPTH, D_RNN)),
        "w_rg_a": nrm(ks[6], (DEPTH, N_RNN_BLOCKS, RNN_BLOCK, RNN_BLOCK), RNN_BLOCK),
        "b_rg_a": bias(ks[7], (DEPTH, D_RNN)),
        "w_rg_x": nrm(ks[8], (DEPTH, N_RNN_BLOCKS, RNN_BLOCK, RNN_BLOCK), RNN_BLOCK),
        "b_rg_x": bias(ks[10], (DEPTH, D_RNN)),
        "rg_lambda": rg_lambda,
        "g_attn_out": gain(ks[11], (DEPTH, D_ATTN)),
        "g_rnn_out": gain(ks[12], (DEPTH, D_RNN)),
        "w_out": nrm(ks[13], (DEPTH, D_MIX, D_MODEL), D_MIX),
        "g_ffn": gain(ks[14], (DEPTH, D_MODEL)),
        "w_ffn_gate": nrm(ks[15], (DEPTH, D_MODEL, D_FF), D_MODEL),
        "w_ffn_up": nrm(ks[16], (DEPTH, D_MODEL, D_FF), D_MODEL),
        "w_ffn_down": nrm(ks[17], (DEPTH, D_FF, D_MODEL), D_FF),
        "g_ple": gain(ks[18], (DEPTH, D_MODEL)),
        "w_ple_gate": nrm(ks[19], (DEPTH, D_MODEL, D_MODEL), D_MODEL),
        "w_ple_proj": nrm(ks[20], (DEPTH, PLE_DIM, D_MODEL), PLE_DIM),
        "g_ple_out": gain(ks[21], (DEPTH, D_MODEL)),
        "g_final": gain(ks[22], (D_MODEL,)),
    }


def reference(x, p, g_mix, w_in, conv_w, conv_b, w_rg_a, b_rg_a, w_rg_x, b_rg_x,
              rg_lambda, g_attn_out, g_rnn_out, w_out, g_ffn, w_ffn_gate, w_ffn_up,
              w_ffn_down, g_ple, w_ple_gate, w_ple_proj, g_ple_out, g_final):
    B, S, _ = x.shape
    split_pts = [D_ATTN, 2 * D_ATTN, 3 * D_ATTN, 3 * D_ATTN + D_RNN]
    h = x
    for l in range(DEPTH):
        u = rms_norm(h, g_mix[l])
        proj = u @ w_in[l]
        q, k, v, xr, gr = jnp.split(proj, split_pts, axis=-1)
        q = q.reshape(B, S, N_ATTN_HEADS, D_HEAD)
        k = k.reshape(B, S, N_ATTN_HEADS, D_HEAD)
        v = v.reshape(B, S, N_ATTN_HEADS, D_HEAD)
        attn = stick_breaking_attention(q, k, v).reshape(B, S, D_ATTN)

        xr = causal_depthwise_conv(xr, conv_w[l], conv_b[l])
        rec = rg_lru(xr, w_rg_a[l], b_rg_a[l], w_rg_x[l], b_rg_x[l], rg_lambda[l])
        rec = rec * jax.nn.gelu(gr, approximate=True)

        mixed = jnp.concatenate([rms_norm(attn, g_attn_out[l]),
                                 rms_norm(rec, g_rnn_out[l])], axis=-1)
        h = h + mixed @ w_out[l]

        f = rms_norm(h, g_ffn[l])
        h = h + (jax.nn.silu(f @ w_ffn_gate[l]) * (f @ w_ffn_up[l])) @ w_ffn_down[l]

        pe = rms_norm(p[l] @ w_ple_proj[l], g_ple_out[l])
        gate = jax.nn.sigmoid(rms_norm(h, g_ple[l]) @ w_ple_gate[l])
        h = h + gate * pe
    return rms_norm(h, g_final)
```

```python
import contextlib
import numpy as np
import concourse.bass as bass
import concourse.mybir as mybir
from concourse.bass_utils import run_bass_kernel_spmd

F32 = mybir.dt.float32
BF16 = mybir.dt.bfloat16
AF = mybir.ActivationFunctionType
ALU = mybir.AluOpType
EPS = 1e-6
SAME_SYNC = True
FULL_SELF_SYNC = False


class Cfg:
    def __init__(self, D=4096, T=2048, C=2048, DFF=11008, PLE=256, NFFG=4, debug=False, stages="0cABRC"):
        self.stages = stages
        self.D, self.T, self.C, self.DFF, self.PLE, self.NFFG, self.debug = D, T, C, DFF, PLE, NFFG, debug
        self.DA = D // 2
        self.DR = D // 2
        self.H = self.DA // 128
        self.NB = self.DR // 128
        self.KC = D // 128
        self.DIN = 3 * self.DA + 2 * self.DR
        self.NTOK = C + T
        self.NBLK = self.NTOK // 128
        self.NST = T // 512
        self.NSTA = self.NTOK // 512
        self.FFC = DFF // 128
        self.PC = PLE // 128
        o = 0
        self.pv = {}
        for name, n in [("gmix", self.KC), ("gffn", self.KC), ("gple", self.KC), ("cw", self.NB * 4),
                        ("cb", self.NB), ("ba", self.NB), ("bx", self.NB), ("lam", self.NB),
                        ("gA", self.H), ("gR", self.NB)]:
            self.pv[name] = (o, n)
            o += n
        self.NPV = o
        self.NCONST = 128 * 3 + 4 * 512


class Tok:
    __slots__ = ("sem", "val", "sid")

    def __init__(self, sem, val, sid):
        self.sem, self.val, self.sid = sem, val, sid


class Buf:
    __slots__ = ("name", "w", "r", "pend", "excl", "small")

    def __init__(self, name="", excl=False, small=False):
        self.name = name
        self.excl = excl
        self.small = small
        self.w = {}
        self.r = {}
        self.pend = None


class Eng:
    def __init__(self, K, name, selfsync):
        self.K = K
        self.name = name
        self.sem = K.new_sem("e_" + name)
        self.sid = K.new_sid()
        self.cnt = 0
        self.seen = {}
        self.pending = []
        self.prog = []
        self.selfsync = selfsync

    def wait(self, tok, force=False):
        if tok.sid == self.sid and not (self.selfsync and (force or FULL_SELF_SYNC)):
            return
        if self.seen.get(tok.sid, 0) >= tok.val:
            return
        self.seen[tok.sid] = tok.val
        sem, val = tok.sem, tok.val
        self.prog.append(lambda be: be.wait_ge(sem, val))


class DmaQ:
    def __init__(self, K, eng, nslots, name):
        self.eng = eng
        self.slots = []
        for i in range(nslots):
            self.slots.append([K.new_sem("d_%s%d" % (name, i)), 0, K.new_sid()])
        self.i = 0


class Tracker:
    def __init__(self, nc, stack):
        self.nc = nc
        self.stack = stack
        self._sid = 0
        self.pe = Eng(self, "pe", False)
        self.act = Eng(self, "act", SAME_SYNC)
        self.dve = Eng(self, "dve", SAME_SYNC)
        self.pool = Eng(self, "pool", SAME_SYNC)
        self.sp = Eng(self, "sp", False)
        self.engs = [self.pe, self.act, self.dve, self.pool, self.sp]
        self.qld = DmaQ(self, self.sp, 24, "l")
        self.qst = DmaQ(self, self.sp, 16, "s")
        self.qcast = DmaQ(self, self.pool, 3, "c")
        self.ndma = 0
        self.nops = 0

    def new_sem(self, name):
        return self.stack.enter_context(self.nc.semaphore(name))

    def new_sid(self):
        self._sid += 1
        return self._sid

    def _deps(self, eng, reads, writes, pwrites, force=False):
        for b in reads:
            if b.pend is not None and b.pend is not eng:
                raise RuntimeError("buffer %s has pending unflushed access" % b.name)
            f = force or b.small
            for t in list(b.w.values()):
                eng.wait(t, f)
            if b.excl:
                for t in list(b.r.values()):
                    if t.sid != eng.sid:
                        eng.wait(t)
        for b in writes:
            if b.pend is not None and b.pend is not eng:
                raise RuntimeError("buffer %s has pending unflushed access" % b.name)
            f = force or b.small
            for t in list(b.w.values()):
                eng.wait(t, f)
            for t in list(b.r.values()):
                eng.wait(t, f)
        for b in pwrites:
            if b.pend is not None and b.pend is not eng:
                raise RuntimeError("buffer %s has pending unflushed access" % b.name)
            f = force or b.small
            for t in list(b.r.values()):
                eng.wait(t, f)

    def op(self, eng, fn, reads=(), writes=(), pwrites=(), mark=True, force=False):
        self._deps(eng, reads, writes, pwrites, force)
        self.nops += 1
        eng.pending.append((reads, writes, pwrites))
        for b in reads:
            b.pend = eng
        for b in writes:
            b.pend = eng
        for b in pwrites:
            b.pend = eng
        if mark:
            eng.cnt += 1
            sem = eng.sem
            eng.prog.append(lambda be: fn(be).then_inc(sem, 1))
            tok = Tok(eng.sem, eng.cnt, eng.sid)
            for (r, w, pw) in eng.pending:
                for b in r:
                    b.r[eng.sid] = tok
                    b.pend = None
                for b in w:
                    b.w = {eng.sid: tok}
                    b.r = {}
                    b.pend = None
                for b in pw:
                    b.w[eng.sid] = tok
                    b.pend = None
            eng.pending = []
        else:
            eng.prog.append(lambda be: fn(be))

    def dma(self, q, out, in_, reads=(), writes=(), pwrites=(), **kw):
        eng = q.eng
        assert not eng.pending
        self._deps(eng, reads, writes, pwrites)
        slot = q.slots[q.i % len(q.slots)]
        q.i += 1
        self.ndma += 1
        sem, val, sid = slot
        if val > 0:
            eng.wait(Tok(sem, val, sid))
        slot[1] = val + 16
        eng.prog.append(lambda be: be.dma_start(out=out, in_=in_, **kw).then_inc(sem, 16))
        tok = Tok(sem, val + 16, sid)
        for b in reads:
            b.r[sid] = tok
        for b in writes:
            b.w = {sid: tok}
            b.r = {}
        for b in pwrites:
            b.w[sid] = tok

    def barrier(self):
        toks = []
        for e in self.engs:
            assert not e.pending, e.name
            if e.cnt:
                toks.append(Tok(e.sem, e.cnt, e.sid))
        for q in (self.qld, self.qst, self.qcast):
            for sem, val, sid in q.slots:
                if val:
                    toks.append(Tok(sem, val, sid))
        for e in self.engs:
            for t in toks:
                if t.sid != e.sid:
                    e.wait(t)

    def finish(self):
        self.barrier()


class Rot:
    def __init__(self, items):
        self.items = items
        self.i = 0

    def next(self):
        it = self.items[self.i % len(self.items)]
        self.i += 1
        return it


def build_nc(cfg):
    nc = bass.Bass("TRN2", target_bir_lowering=False)
    D, T, C, DFF, PLE = cfg.D, cfg.T, cfg.C, cfg.DFF, cfg.PLE
    DA, DR, H, NB, KC, DIN, NTOK, NBLK = cfg.DA, cfg.DR, cfg.H, cfg.NB, cfg.KC, cfg.DIN, cfg.NTOK, cfg.NBLK
    FFC, PC = cfg.FFC, cfg.PC

    def din(name, shape, dt=F32):
        return nc.dram_tensor(name, list(shape), dt, kind="ExternalInput").ap()

    dbg_kind = "ExternalOutput" if cfg.debug else "Internal"

    def dscr(name, shape, dt, dbg=False):
        return nc.dram_tensor(name, list(shape), dt, kind=(dbg_kind if dbg else "Internal")).ap()

    x_d = din("x", [NTOK, D])
    p_d = din("p", [T, PLE])
    flag_d = din("flag", [128, 1])
    pv_d = din("pv", [128, cfg.NPV])
    const_d = din("consts", [128, cfg.NCONST])
    gpo_d = din("gpo", [1, D])
    gfin_d = din("gfin", [1, D])
    w_in_d = din("w_in", [D, DIN])
    w_out_d = din("w_out", [D, D])
    w_gate_d = din("w_gate", [D, DFF])
    w_up_d = din("w_up", [D, DFF])
    w_down_d = din("w_down", [DFF, D])
    w_pg_d = din("w_pg", [D, D])
    w_pp_d = din("w_pp", [PLE, D])
    w_ra_d = din("w_ra", [NB * 128, 128])
    w_rx_d = din("w_rx", [NB * 128, 128])
    out_d = nc.dram_tensor("out", [T, D], F32, kind="ExternalOutput").ap()

    wb_in = dscr("wb_in", [D, DIN], BF16)
    wb_out = dscr("wb_out", [D, D], BF16)
    wb_gate = dscr("wb_gate", [D, DFF], BF16)
    wb_up = dscr("wb_up", [D, DFF], BF16)
    wb_down = dscr("wb_down", [DFF, D], BF16)
    wb_pg = dscr("wb_pg", [D, D], BF16)
    wb_pp = dscr("wb_pp", [PLE, D], BF16)
    wb_ra = dscr("wb_ra", [NB * 128, 128], BF16)
    wb_rx = dscr("wb_rx", [NB * 128, 128], BF16)
    qT_d = dscr("qT", [H, 128, T], BF16, True)
    kT_d = dscr("kT", [H, 128, NTOK], BF16, True)
    vS_d = dscr("vS", [H, 128, NBLK, 128], BF16, True)
    xrT_d = dscr("xrT", [NB, 128, NTOK], F32, True)
    grT_d = dscr("grT", [NB, 128, T], F32, True)
    mixT_d = dscr("mixT", [H + NB, 128, T], BF16, True)
    ss_d = dscr("ssdbg", [128, 2 * (T // 128)], F32, True)
    hS_d = dscr("hS", [T, D], F32, True)

    with contextlib.ExitStack() as gstack:
        K = Tracker(nc, gstack)
        PE, ACT, DVE, POOL, SP = K.pe, K.act, K.dve, K.pool, K.sp
        QL, QS, QC = K.qld, K.qst, K.qcast

        def sb(stack, name, shape, dt):
            return stack.enter_context(nc.sbuf_tensor("s_" + name, list(shape), dt))

        psT = []
        for i in range(2):
            t = gstack.enter_context(nc.psum_tensor("psT%d" % i, [128, 1024], BF16))
            psT.append((t, Buf("psT%d" % i, excl=True)))
        psF = []
        psBig = []
        for i in range(3):
            t = gstack.enter_context(nc.psum_tensor("psB%d" % i, [128, 1024], F32))
            psBig.append(t)
            for h_ in range(2):
                psF.append((t[:, h_ * 512:(h_ + 1) * 512], Buf("psF%d" % (2 * i + h_), excl=True)))
        psT_rot = Rot(psT)
        psF_rot = Rot(psF)

        cst32 = sb(gstack, "cst32", [128, cfg.NCONST], F32)
        cst32_b = Buf("cst32")
        cbf = sb(gstack, "cbf", [128, 384], BF16)
        cbf_b = Buf("cbf")
        pvt = sb(gstack, "pvt", [128, cfg.NPV], F32)
        pvt_b = Buf("pvt")
        flag_t = sb(gstack, "flag", [128, 1], F32)
        flag_b = Buf("flag")
        ssA = sb(gstack, "ssA", [128, T // 128], F32)
        ssR = sb(gstack, "ssR", [128, T // 128], F32)
        ss_b = Buf("ss", small=True)
        coef = sb(gstack, "coef", [128, 2 * NB], F32)
        coef_b = Buf("coef", small=True)

        K.dma(QL, cst32[:, :], const_d[:, :], writes=[cst32_b])
        K.dma(QL, pvt[:, :], pv_d[:, :], writes=[pvt_b])
        K.dma(QL, flag_t[:, :], flag_d[:, :], writes=[flag_b])
        K.op(DVE, lambda be: be.tensor_copy(out=cbf[:, :], in_=cst32[:, 0:384]), reads=[cst32_b], writes=[cbf_b])
        K.op(DVE, lambda be: be.memset(ssA[:, :], 0.0), writes=[ss_b])
        K.op(DVE, lambda be: be.memset(ssR[:, :], 0.0), pwrites=[ss_b])
        ident = cbf[:, 0:128]
        Lmat = cbf[:, 128:256]
        ones = cbf[:, 256:384]

        def maskM(r):
            return cst32[:, 384 + r * 512: 384 + (r + 1) * 512]

        def pv(name, j0=0, n=None):
            o, nn = cfg.pv[name]
            if n is None:
                n = nn - j0
            return pvt[:, o + j0: o + j0 + n]

        lo, ln_ = cfg.pv["lam"]
        if 'c' not in cfg.stages:
            K_op_real = K.op
            K.op = lambda *a, **k: None
        K.op(ACT, lambda be: be.activation(out=coef[:, 0:NB], in_=pvt[:, lo:lo + NB], func=AF.Exp, scale=-1.0),
             reads=[pvt_b], writes=[coef_b])
        K.op(ACT, lambda be: be.activation(out=coef[:, 0:NB], in_=coef[:, 0:NB], func=AF.Ln, bias=1.0),
             reads=[coef_b], writes=[coef_b])
        K.op(DVE, lambda be: be.tensor_scalar(out=coef[:, NB:2 * NB], in0=coef[:, 0:NB], scalar1=-16.0, scalar2=None,
                                              op0=ALU.mult), reads=[coef_b], pwrites=[coef_b])
        K.op(DVE, lambda be: be.tensor_scalar(out=coef[:, 0:NB], in0=coef[:, 0:NB], scalar1=-8.0, scalar2=None,
                                              op0=ALU.mult), reads=[coef_b], writes=[coef_b])

        if 'c' not in cfg.stages:
            K.op = K_op_real
        wbufs = {}
        cast_list = []
        cast_ptr = [0]
        b_tot = {}
        b_done = {}

        def cast_tick(n, paced=True):
            for _ in range(n):
                if cast_ptr[0] >= len(cast_list):
                    return
                d, s_, b = cast_list[cast_ptr[0]]
                cast_ptr[0] += 1
                if paced and PE.cnt > 0:
                    POOL.wait(Tok(PE.sem, PE.cnt, PE.sid))
                K.dma(QC, d, s_, pwrites=[b])
                b_done[id(b)] = b_done.get(id(b), 0) + 1

        def cast_ready(b):
            return b_done.get(id(b), 0) == b_tot.get(id(b), 0)

        def cast_flush():
            cast_tick(len(cast_list), paced=True)

        def cast_weight(name, src, dst, rows, cols, c0=0, ncol=None):
            b = Buf("wb_" + name)
            wbufs[name] = b
            if ncol is None:
                ncol = cols
            cw = ncol
            while cw > 2048:
                for dv in (2, 3, 5, 7, 43):
                    if cw % dv == 0:
                        cw //= dv
                        break
                else:
                    raise ValueError
            rstep = max(1, min(rows, (2 << 20) // (ncol * 4)))
            for r0 in range(0, rows, rstep):
                r1 = min(rows, r0 + rstep)
                s = src[r0:r1, c0:c0 + ncol].rearrange("r (a b) -> r a b", b=cw)
                d = dst[r0:r1, c0:c0 + ncol].rearrange("r (a b) -> r a b", b=cw)
                cast_list.append((d, s, b))
                b_tot[id(b)] = b_tot.get(id(b), 0) + 1

        if "0" not in cfg.stages:
            cast_weight = lambda *a, **k: None
        cast_weight("in_k", w_in_d, wb_in, D, DIN, DA, DA)
        cast_weight("in_v", w_in_d, wb_in, D, DIN, 2 * DA, DA)
        cast_weight("in_xr", w_in_d, wb_in, D, DIN, 3 * DA, DR)
        cast_weight("in_q", w_in_d, wb_in, D, DIN, 0, DA)
        cast_weight("in_gr", w_in_d, wb_in, D, DIN, 3 * DA + DR, DR)
        cast_weight("ra", w_ra_d, wb_ra, NB * 128, 128)
        cast_weight("rx", w_rx_d, wb_rx, NB * 128, 128)
        cast_weight("out", w_out_d, wb_out, D, D)
        cast_weight("gate", w_gate_d, wb_gate, D, DFF)
        cast_weight("up", w_up_d, wb_up, D, DFF)
        cast_weight("down", w_down_d, wb_down, DFF, D)
        cast_weight("pg", w_pg_d, wb_pg, D, D)
        cast_weight("pp", w_pp_d, wb_pp, PLE, D)
        n0 = sum(b_tot[id(wbufs[k_])] for k_ in ("in_k",)) if "0" in cfg.stages else 0
        cast_tick(n0, paced=False)

        RN = 4
        evac_flip = [0]

        class Stage:
            def __init__(self, name, need_rows=True):
                self.stack = contextlib.ExitStack()
                self.name = name

            def __enter__(self):
                self.stack.__enter__()
                return self

            def __exit__(self, *a):
                K.barrier()
                return self.stack.__exit__(*a)

            def tile(self, name, shape, dt, small=False):
                return sb(self.stack, self.name + "_" + name, shape, dt), Buf(self.name + "_" + name, small=small)

            def rot(self, name, shape, dt, n, small=False):
                return Rot([self.tile("%s%d" % (name, i), shape, dt, small) for i in range(n)])

        def make_ring(st):
            return [st.tile("ring%d" % i, [128, 8192], BF16) for i in range(RN)]

        ring_ctr = [0]
        cast_every = [8]

        def cast_flush_buf(b):
            while not cast_ready(b):
                cast_tick(1)

        def stream(ring, jobs):
            flat = []
            starts = []
            for slabs, _ in jobs:
                starts.append(len(flat))
                flat.extend(slabs)
            base = ring_ctr[0]
            ring_ctr[0] += len(flat)
            issued = 0
            ents = {}
            for ji, (slabs, compute) in enumerate(jobs):
                lim = min(len(flat), starts[ji] + RN)
                while issued < lim:
                    src, shape, wbuf = flat[issued]
                    tl, bf = ring[(base + issued) % RN]
                    n = 1
                    for s_ in shape[1:]:
                        n *= s_
                    if len(shape) == 3:
                        view = tl[:, 0:n].rearrange("p (k n) -> p k n", n=shape[2])
                    else:
                        view = tl[:, 0:n]
                    if not cast_ready(wbuf):
                        cast_flush_buf(wbuf)
                    K.dma(QL, view, src, reads=[wbuf], writes=[bf])
                    ents[issued] = (view, bf)
                    issued += 1
                if ji % cast_every[0] == 0:
                    cast_tick(1)
                compute([ents[starts[ji] + i] for i in range(len(slabs))])
                for i in range(len(slabs)):
                    del ents[starts[ji] + i]

        def gemm_f(wb, wbuf, Kc, col0, ncols, actT, actT_b, evac, pair=None):
            wv = wb.rearrange("(k p) n -> p k n", p=128)
            jobs = []
            c = col0
            jg0 = 0
            while c < col0 + ncols:
                w_ = min(256, col0 + ncols - c)
                slabs = [(wv[:, 0:Kc, c:c + w_], [128, Kc, w_], wbuf)]
                if pair is not None:
                    wv2 = pair[0].rearrange("(k p) n -> p k n", p=128)
                    slabs.append((wv2[:, 0:Kc, c:c + w_], [128, Kc, w_], pair[1]))

                def compute(entries, jg0=jg0, w_=w_):
                    for j in range(w_ // 128):
                        pss = []
                        for (view, bf) in entries:
                            ps, psb = psF_rot.next()
                            for kc in range(Kc):
                                K.op(PE, lambda be, ps=ps, view=view, kc=kc, j=j: be.matmul(
                                    ps[:, :], lhsT=view[:, kc, j * 128:(j + 1) * 128], rhs=actT[:, kc, :],
                                    start=(kc == 0), stop=(kc == Kc - 1)),
                                    reads=[bf, actT_b], writes=[psb], mark=(kc == Kc - 1))
                            pss.append((ps, psb))
                        evac(jg0 + j, pss)
                jobs.append((slabs, compute))
                jg0 += w_ // 128
                c += w_
            return jobs

        def gemm_t(wb, wbuf, row0, nK, col0, ncols, actT, actT_b, kc_off, evac, pre=None):
            pieces = []
            k0 = 0
            while k0 < nK:
                n = min(16, nK - k0)
                pieces.append((k0, n))
                k0 += n
            jobs = []
            NO = ncols // 512
            bankd = {}
            for o in range(NO):
                for pi, (k0, n) in enumerate(pieces):
                    r0 = row0 + k0 * 128
                    src = wb[r0:r0 + n * 128, col0 + o * 512: col0 + (o + 1) * 512].rearrange("(k p) n -> p k n", p=128)
                    slabs = [(src, [128, n, 512], wbuf)]

                    def compute(entries, o=o, pi=pi, k0=k0, n=n):
                        view, bf = entries[0]
                        if pi == 0:
                            bankd[o] = [psF_rot.next() for _ in range(4)]
                            if pre is not None:
                                pre(o)
                        banks = bankd[o]
                        for tt in range(4):
                            ps, psb = banks[tt]
                            for kl in range(n):
                                K.op(PE, lambda be, ps=ps, view=view, kl=kl, tt=tt, k0=k0, pi=pi, n=n: be.matmul(
                                    ps[:, :], lhsT=actT[:, kc_off + k0 + kl, tt * 128:(tt + 1) * 128], rhs=view[:, kl, :],
                                    start=(pi == 0 and kl == 0), stop=(pi == len(pieces) - 1 and kl == n - 1)),
                                    reads=[bf, actT_b], writes=[psb], mark=(kl == n - 1))
                            if pi == len(pieces) - 1:
                                evac(tt, o, ps, psb)
                    jobs.append((slabs, compute))
            return jobs

        def norm_transpose(st, src_rows, src_bufs, gname, actT, actT_b, tt, rows_rot, xn_rot, sm_rot):
            hrow, hrow_b = rows_rot.next()
            xn, xn_b = xn_rot.next()
            sm, sm_b = sm_rot.next()
            K.dma(QL, hrow[:, :], src_rows, reads=src_bufs, writes=[hrow_b])
            K.op(ACT, lambda be: be.activation(out=xn[:, :], in_=hrow[:, :], func=AF.Square, accum_out=sm[:, 0:1]),
                 reads=[hrow_b], writes=[xn_b, sm_b])
            K.op(DVE, lambda be: be.tensor_scalar(out=sm[:, 1:2], in0=sm[:, 0:1], scalar1=1.0 / D, scalar2=EPS,
                                                  op0=ALU.mult, op1=ALU.add), reads=[sm_b], writes=[sm_b])
            K.op(ACT, lambda be: be.activation(out=sm[:, 2:3], in_=sm[:, 1:2], func=AF.Ln), reads=[sm_b], writes=[sm_b])
            K.op(ACT, lambda be: be.activation(out=sm[:, 2:3], in_=sm[:, 2:3], func=AF.Exp, scale=-0.5), reads=[sm_b], writes=[sm_b])
            K.op(DVE, lambda be: be.tensor_scalar(out=xn[:, :], in0=hrow[:, :], scalar1=sm[:, 2:3], scalar2=None,
                                                  op0=ALU.mult), reads=[hrow_b, sm_b], writes=[xn_b])
            G = min(8, KC)
            go, _ = cfg.pv[gname]
            for g0 in range(0, KC, G):
                pt, ptb = psT_rot.next()
                for j in range(G):
                    kc = g0 + j
                    K.op(PE, lambda be, pt=pt, j=j, kc=kc: be.transpose(
                        out=pt[:, j * 128:(j + 1) * 128], in_=xn[:, kc * 128:(kc + 1) * 128], identity=ident),
                        reads=[xn_b, cbf_b], writes=[ptb], mark=(j == G - 1))
                gv = pvt[:, go + g0: go + g0 + G].unsqueeze(2).to_broadcast([128, G, 128])
                K.op(DVE, lambda be, pt=pt, g0=g0, gv=gv: be.tensor_tensor(
                    out=actT[:, g0:g0 + G, tt * 128:(tt + 1) * 128],
                    in0=pt[:, 0:G * 128].rearrange("p (g t) -> p g t", t=128), in1=gv, op=ALU.mult),
                    reads=[ptb, pvt_b], pwrites=[actT_b])

        def evac_copy(dst, ps, reads, writes=(), pwrites=()):
            evac_flip[0] ^= 1
            if evac_flip[0]:
                K.op(ACT, lambda be: be.activation(out=dst, in_=ps, func=AF.Copy), reads=reads, writes=writes, pwrites=pwrites)
            else:
                K.op(DVE, lambda be: be.tensor_copy(out=dst, in_=ps), reads=reads, writes=writes, pwrites=pwrites)

        qT_b = [Buf("qT%d" % h) for h in range(H)]
        kT_b = [Buf("kT%d" % h) for h in range(H)]
        vS_b = [Buf("vS%d" % h) for h in range(H)]
        xrT_b = [Buf("xrT%d" % c) for c in range(NB)]
        grT_b = [Buf("grT%d" % c) for c in range(NB)]
        mixT_b = [Buf("mixT%d" % c) for c in range(H + NB)]
        hS_b = {}

        def hb(st_, tt, o):
            k_ = (st_, tt, o)
            if k_ not in hS_b:
                hS_b[k_] = Buf("hS%d_%d_%d" % k_)
            return hS_b[k_]

        with (Stage("A") if "A" in cfg.stages else contextlib.nullcontext()) as st:
          if st is not None:
              ring = make_ring(st)
              rows_rot = st.rot("row", [128, D], F32, 2)
              xn_rot = st.rot("xn", [128, D], BF16, 2)
              sm_rot = st.rot("sm", [128, 4], F32, 4, small=True)
              uT_rot = st.rot("uT", [128, KC, 512], BF16, 2)
              ob_rot = st.rot("ob", [128, 512], BF16, 4)
              of_rot = st.rot("of", [128, 512], F32, 4)
              uTs = [uT_rot.next() for _ in range(cfg.NSTA)]

              def mk_evs(g0, t0):
                  def ev_q(jg, pss):
                      (ps, psb), = pss
                      ob, obb = ob_rot.next()
                      evac_copy(ob[:, :], ps[:, :], [psb], writes=[obb])
                      K.dma(QS, qT_d[jg, :, t0:t0 + 512], ob[:, :], reads=[obb], pwrites=[qT_b[jg]])

                  def ev_k(jg, pss):
                      (ps, psb), = pss
                      ob, obb = ob_rot.next()
                      evac_copy(ob[:, :], ps[:, :], [psb], writes=[obb])
                      K.dma(QS, kT_d[jg, :, g0:g0 + 512], ob[:, :], reads=[obb], pwrites=[kT_b[jg]])

                  def ev_xr(jg, pss):
                      (ps, psb), = pss
                      of, ofb = of_rot.next()
                      evac_copy(of[:, :], ps[:, :], [psb], writes=[ofb])
                      K.dma(QS, xrT_d[jg, :, g0:g0 + 512], of[:, :], reads=[ofb], pwrites=[xrT_b[jg]])

                  def ev_gr(jg, pss):
                      (ps, psb), = pss
                      of, ofb = of_rot.next()
                      evac_copy(of[:, :], ps[:, :], [psb], writes=[ofb])
                      K.dma(QS, grT_d[jg, :, t0:t0 + 512], of[:, :], reads=[ofb], pwrites=[grT_b[jg]])

                  def ev_v(tt, o, ps, psb):
                      ob, obb = ob_rot.next()
                      evac_copy(ob[:, :], ps[:, :], [psb], writes=[obb])
                      blk = (g0 // 128) + tt
                      nh = 512 // 128
                      dst = vS_d[o * nh:(o + 1) * nh, :, blk, :].rearrange("h p d -> p h d")
                      K.dma(QS, dst, ob[:, :].rearrange("p (h d) -> p h d", d=128), reads=[obb],
                            pwrites=[vS_b[o * nh + i] for i in range(nh)])
                  return ev_q, ev_k, ev_xr, ev_gr, ev_v

              def norm_job(sa):
                  def f(ents):
                      uT, uT_b = uTs[sa]
                      for tt in range(4):
                          norm_transpose(st, x_d[sa * 512 + tt * 128: sa * 512 + (tt + 1) * 128, :], [], "gmix", uT, uT_b, tt,
                                         rows_rot, xn_rot, sm_rot)
                  return ([], f)

              jobs = [norm_job(0)]
              for sa in range(cfg.NSTA):
                  g0 = sa * 512
                  is_ctx = g0 < C
                  t0 = g0 - C
                  uT, uT_b = uTs[sa]
                  ev_q, ev_k, ev_xr, ev_gr, ev_v = mk_evs(g0, t0)
                  jobs += gemm_f(wb_in, wbufs["in_k"], KC, DA, DA, uT, uT_b, ev_k)
                  if sa + 1 < cfg.NSTA:
                      jobs.append(norm_job(sa + 1))
                  jobs += gemm_t(wb_in, wbufs["in_v"], 0, KC, 2 * DA, DA, uT, uT_b, 0, ev_v)
                  jobs += gemm_f(wb_in, wbufs["in_xr"], KC, 3 * DA, DR, uT, uT_b, ev_xr)
                  if not is_ctx:
                      jobs += gemm_f(wb_in, wbufs["in_q"], KC, 0, DA, uT, uT_b, ev_q)
                      jobs += gemm_f(wb_in, wbufs["in_gr"], KC, 3 * DA + DR, DR, uT, uT_b, ev_gr)
              stream(ring, jobs)

        scale = 128.0 ** -0.5
        with (Stage("B") if "B" in cfg.stages else contextlib.nullcontext()) as st:
          if st is not None:
              assert H % 2 == 0
              kv_rot = Rot([(st.tile("kT%d" % i, [128, NTOK], BF16), st.tile("vh%d" % i, [128, NBLK, 128], BF16),
                             st.tile("qT%d" % i, [128, T], BF16)) for i in range(4)])
              e_rot = st.rot("e", [128, 1024], F32, 3)
              sp_rot = st.rot("sp", [128, 1024], BF16, 3)
              ec_rot = st.rot("ec", [128, 1024], F32, 2)
              w_rot = st.rot("w", [128, 1024], BF16, 3)
              R_rot = st.rot("R", [128, 1024], BF16, 2)
              sq_rot = st.rot("sq", [128, 512], BF16, 2)
              mo_rot = st.rot("mo", [128, 512], BF16, 2)
              S2, S2b = psBig[0], [psF[0][1], psF[1][1]]
              C2, C2b = psBig[1], [psF[2][1], psF[3][1]]
              Ob = [psF[4], psF[5]]
              pss_t, pss_b = psT[0]
              pss32 = pss_t[:, :].bitcast(F32)

              iters = []
              for hp in range(H // 2):
                  for qt in range(T // 512):
                      nkb = (C + (qt + 1) * 512) // 128
                      for it, kb in enumerate(reversed(range(nkb))):
                          iters.append(dict(hp=hp, qt=qt, it=it, kb=kb, n=nkb, r=kb - (C // 128 + 4 * qt)))
              NI = len(iters)
              hd = {}

              def load_head(h):
                  (kt, ktb), (vh, vhb), (qh, qhb) = kv_rot.next()
                  K.dma(QL, kt[:, :], kT_d[h, :, :], reads=[kT_b[h]], writes=[ktb])
                  K.dma(QL, vh[:, :, :], vS_d[h, :, :, :], reads=[vS_b[h]], writes=[vhb])
                  K.dma(QL, qh[:, :], qT_d[h, :, :], reads=[qT_b[h]], writes=[qhb])
                  hd[h] = (kt, ktb, vh, vhb, qh, qhb)

              state = {}
              for i in range(NI + 2):
                  if i % 3 == 0:
                      cast_tick(1)
                  if i < NI:
                      d = iters[i]
                      hp, qt = d["hp"], d["qt"]
                      if d["it"] == 0 and qt == 0 and hp == 0:
                          load_head(0)
                          load_head(1)
                      if d["it"] == 3 and qt == 0 and hp + 1 < H // 2:
                          load_head(2 * hp + 2)
                          load_head(2 * hp + 3)
                      for hh in range(2):
                          kt, ktb, vh, vhb, qh, qhb = hd[2 * hp + hh]
                          K.op(PE, lambda be, kt=kt, qh=qh, d=d, hh=hh: be.matmul(
                              S2[:, hh * 512:(hh + 1) * 512], lhsT=kt[:, d["kb"] * 128:(d["kb"] + 1) * 128],
                              rhs=qh[:, d["qt"] * 512:(d["qt"] + 1) * 512], start=True, stop=True),
                              reads=[ktb, qhb], writes=[S2b[hh]])
                      e, eb = e_rot.next()
                      K.op(ACT, lambda be, e=e: be.activation(out=e[:, :], in_=S2[:, :], func=AF.Exp, scale=scale),
                           reads=S2b, writes=[eb])
                      if d["r"] >= 0:
                          K.op(DVE, lambda be, e=e, r=d["r"]: be.tensor_tensor(
                              out=e[:, :].rearrange("p (h t) -> p h t", h=2), in0=e[:, :].rearrange("p (h t) -> p h t", h=2),
                              in1=maskM(r).unsqueeze(1).to_broadcast([128, 2, 512]), op=ALU.mult),
                              reads=[eb, cst32_b], writes=[eb])
                      sp, spb = sp_rot.next()
                      K.op(ACT, lambda be, e=e, sp=sp: be.activation(out=sp[:, :], in_=e[:, :], func=AF.Ln, bias=1.0),
                           reads=[eb], writes=[spb])
                      Rn, Rnb = R_rot.next()
                      if d["it"] == 0:
                          K.op(DVE, lambda be, Rn=Rn: be.memset(Rn[:, :], 0.0), writes=[Rnb])
                      else:
                          pst = state[i - 1]
                          K.op(DVE, lambda be, Rn=Rn, pst=pst: be.tensor_tensor(out=Rn[:, :], in0=pst["R"][:, :], in1=pst["sp"][:, :],
                                                                          op=ALU.add),
                               reads=[pst["Rb"], pst["spb"]], writes=[Rnb])
                      state[i] = dict(e=e, eb=eb, sp=sp, spb=spb, R=Rn, Rb=Rnb, d=d)
                  if 0 <= i - 1 < NI:
                      s1 = state[i - 1]
                      for hh in range(2):
                          cs = slice(hh * 512, (hh + 1) * 512)
                          K.op(PE, lambda be, s1=s1, cs=cs: be.matmul(C2[:, cs], lhsT=Lmat, rhs=s1["sp"][:, cs], start=True, stop=False),
                               reads=[cbf_b, s1["spb"]], writes=[C2b[hh]], mark=False)
                          K.op(PE, lambda be, s1=s1, cs=cs: be.matmul(C2[:, cs], lhsT=ones, rhs=s1["R"][:, cs], start=False, stop=True),
                               reads=[cbf_b, s1["Rb"]], writes=[C2b[hh]])
                      ec, ecb = ec_rot.next()
                      K.op(ACT, lambda be, ec=ec: be.activation(out=ec[:, :], in_=C2[:, :], func=AF.Exp, scale=-1.0),
                           reads=C2b, writes=[ecb])
                      w, wb_ = w_rot.next()
                      K.op(DVE, lambda be, w=w, s1=s1, ec=ec: be.tensor_tensor(out=w[:, :], in0=s1["e"][:, :], in1=ec[:, :], op=ALU.mult),
                           reads=[s1["eb"], ecb], writes=[wb_])
                      s1["w"], s1["wb"] = w, wb_
                  if 0 <= i - 2 < NI:
                      s2 = state.pop(i - 2)
                      d2 = s2["d"]
                      last = d2["it"] == d2["n"] - 1
                      for hh in range(2):
                          h2 = 2 * d2["hp"] + hh
                          kt, ktb, vh, vhb, qh, qhb = hd[h2]
                          po, pob = Ob[hh]
                          cs = slice(hh * 512, (hh + 1) * 512)
                          K.op(PE, lambda be, po=po, vh=vh, s2=s2, d2=d2, last=last, cs=cs: be.matmul(
                              po[:, :], lhsT=vh[:, d2["kb"], :], rhs=s2["w"][:, cs], start=(d2["it"] == 0), stop=last),
                              reads=[vhb, s2["wb"]], writes=[pob], mark=True)
                          if last:
                              qt2 = d2["qt"]
                              sq, sqb = sq_rot.next()
                              mo, mob = mo_rot.next()
                              go, _ = cfg.pv["gA"]
                              K.op(DVE, lambda be, mo=mo, po=po, h2=h2, go=go: be.tensor_scalar(
                                  out=mo[:, :], in0=po[:, :], scalar1=pvt[:, go + h2:go + h2 + 1], scalar2=None, op0=ALU.mult),
                                  reads=[pob, pvt_b], writes=[mob])
                              K.op(ACT, lambda be, sq=sq, po=po: be.activation(out=sq[:, :], in_=po[:, :], func=AF.Square),
                                   reads=[pob, mob], writes=[sqb])
                              K.dma(QS, mixT_d[h2, :, qt2 * 512:(qt2 + 1) * 512], mo[:, :], reads=[mob], pwrites=[mixT_b[h2]])
                              for tq in range(4):
                                  K.op(PE, lambda be, sq=sq, tq=tq: be.matmul(pss32[:, tq:tq + 1], lhsT=sq[:, tq * 128:(tq + 1) * 128],
                                                                           rhs=ones[:, 0:1], start=True, stop=True),
                                       reads=[sqb, cbf_b], writes=[pss_b], mark=(tq == 3))
                              K.op(DVE, lambda be, qt2=qt2: be.tensor_tensor(out=ssA[:, qt2 * 4:qt2 * 4 + 4], in0=ssA[:, qt2 * 4:qt2 * 4 + 4],
                                                                          in1=pss32[:, 0:4], op=ALU.add),
                                   reads=[pss_b, ss_b], writes=[ss_b])

        cast_flush_buf(wbufs["ra"])
        cast_flush_buf(wbufs["rx"])
        with (Stage("R") if "R" in cfg.stages else contextlib.nullcontext()) as st:
          if st is not None:
              L = max(C, T)
              assert C % 512 == 0 and T % 512 == 0
              set_rot = Rot([dict(XR=st.tile("XR%d" % i, [128, L + 4], F32), Y=st.tile("Y%d" % i, [128, L], F32),
                                  YB=st.tile("YB%d" % i, [128, L], BF16), Rb=st.tile("Rb%d" % i, [128, L], F32),
                                  Ab=st.tile("Ab%d" % i, [128, L], F32), Ib=st.tile("Ib%d" % i, [128, L], F32)) for i in range(3)])
              GR_rot = st.rot("GR", [128, T], F32, 3)
              GT, GT_b = st.tile("GT", [128, T], F32)
              Hs, Hs_b = st.tile("Hs", [128, T], F32)
              SQ_rot = st.rot("SQ", [128, T], BF16, 2)
              MO_rot = st.rot("MO", [128, T], BF16, 2)
              Wa, Wa_b = st.tile("Wa", [128, NB, 128], BF16)
              Wx, Wx_b = st.tile("Wx", [128, NB, 128], BF16)
              h0_rot = st.rot("h0", [128, 2], F32, 3, small=True)
              pss_t, pss_b = psT[0]
              pss32 = pss_t[:, :].bitcast(F32)
              K.dma(QL, Wa[:, :, :], wb_ra.rearrange("(n i) j -> i n j", i=128), reads=[wbufs["ra"]], writes=[Wa_b])
              K.dma(QL, Wx[:, :, :], wb_rx.rearrange("(n i) j -> i n j", i=128), reads=[wbufs["rx"]], writes=[Wx_b])
              cwo, _ = cfg.pv["cw"]
              cbo, _ = cfg.pv["cb"]
              bao, _ = cfg.pv["ba"]
              bxo, _ = cfg.pv["bx"]
              gRo, _ = cfg.pv["gR"]
              units = []
              for c in range(NB):
                  if C > 0:
                      units.append((c, 0, 0, C))
                  units.append((c, 1, C, T))
              ust = {}
              h0s = {}

              def front_a1(u):
                  c, sgi, s0, n = units[u]
                  S_ = set_rot.next()
                  (XR, XR_b), (Y, Y_b), (YB, YB_b) = S_["XR"], S_["Y"], S_["YB"]
                  (Rb_, Rb_b), (Ab_, Ab_b), (Ib_, Ib_b) = S_["Rb"], S_["Ab"], S_["Ib"]
                  st_ = dict(S_)
                  ust[u] = st_
                  if s0 == 0:
                      K.dma(QL, XR[:, 4:4 + n], xrT_d[c, :, 0:n], reads=[xrT_b[c]], writes=[XR_b])
                      K.op(DVE, lambda be: be.memset(XR[:, 0:4], 0.0), pwrites=[XR_b])
                  else:
                      K.dma(QL, XR[:, 0:4 + n], xrT_d[c, :, s0 - 4:s0 + n], reads=[xrT_b[c]], writes=[XR_b])
                  if sgi == 1:
                      GR, GR_b = GR_rot.next()
                      st_["GR"] = (GR, GR_b)
                      K.dma(QL, GR[:, :], grT_d[c, :, :], reads=[grT_b[c]], writes=[GR_b])
                      G_ = GT[:, 0:T]
                      K.op(ACT, lambda be: be.activation(out=G_, in_=GR[:, :], func=AF.Square), reads=[GR_b], writes=[GT_b])
                      K.op(ACT, lambda be: be.activation(out=G_, in_=G_, func=AF.Identity, scale=0.044715, bias=1.0),
                           reads=[GT_b], writes=[GT_b])
                      K.op(POOL, lambda be: be.tensor_tensor(out=G_, in0=G_, in1=GR[:, :], op=ALU.mult), reads=[GT_b, GR_b], writes=[GT_b])
                  K.op(ACT, lambda be: be.activation(out=Y[:, 0:n], in_=XR[:, 4:4 + n], func=AF.Identity,
                                                     scale=pvt[:, cwo + c * 4 + 3:cwo + c * 4 + 4], bias=pvt[:, cbo + c:cbo + c + 1]),
                       reads=[XR_b, pvt_b], writes=[Y_b])
                  for i_ in range(3):
                      sh = 3 - i_
                      K.op(DVE, lambda be, i_=i_, sh=sh: be.scalar_tensor_tensor(
                          out=Y[:, 0:n], in0=XR[:, 4 - sh:4 - sh + n], scalar=pvt[:, cwo + c * 4 + i_:cwo + c * 4 + i_ + 1],
                          in1=Y[:, 0:n], op0=ALU.mult, op1=ALU.add), reads=[XR_b, pvt_b, Y_b], writes=[Y_b])

              def front_a2(u):
                  c, sgi, s0, n = units[u]
                  st_ = ust[u]
                  (Y, Y_b), (YB, YB_b) = st_["Y"], st_["YB"]
                  K.op(ACT, lambda be: be.activation(out=YB[:, 0:n], in_=Y[:, 0:n], func=AF.Copy), reads=[Y_b], writes=[YB_b])

              def front_b(u):
                  c, sgi, s0, n = units[u]
                  st_ = ust[u]
                  (XR, XR_b), (Y, Y_b), (YB, YB_b) = st_["XR"], st_["Y"], st_["YB"]
                  (Rb_, Rb_b), (Ab_, Ab_b), (Ib_, Ib_b) = st_["Rb"], st_["Ab"], st_["Ib"]
                  if sgi == 1:
                      GR, GR_b = st_["GR"]
                      G_ = GT[:, 0:T]
                      K.op(ACT, lambda be: be.activation(out=G_, in_=G_, func=AF.Sigmoid, scale=1.5957691216057308),
                           reads=[GT_b], writes=[GT_b])
                      K.op(POOL, lambda be: be.tensor_tensor(out=GR[:, :], in0=G_, in1=GR[:, :], op=ALU.mult), reads=[GT_b, GR_b], writes=[GR_b])
                  for tq in range(n // 512):
                      pa, pab = psF_rot.next()
                      K.op(PE, lambda be, pa=pa, tq=tq: be.matmul(pa[:, :], lhsT=Wa[:, c, :], rhs=YB[:, tq * 512:(tq + 1) * 512],
                                                                  start=True, stop=True), reads=[Wa_b, YB_b], writes=[pab])
                      K.op(ACT, lambda be, pa=pa, tq=tq: be.activation(out=Rb_[:, tq * 512:(tq + 1) * 512], in_=pa[:, :], func=AF.Sigmoid,
                                                                       bias=pvt[:, bao + c:bao + c + 1]),
                           reads=[pab, pvt_b], pwrites=[Rb_b])
                      px, pxb = psF_rot.next()
                      K.op(PE, lambda be, px=px, tq=tq: be.matmul(px[:, :], lhsT=Wx[:, c, :], rhs=YB[:, tq * 512:(tq + 1) * 512],
                                                                  start=True, stop=True), reads=[Wx_b, YB_b], writes=[pxb])
                      K.op(ACT, lambda be, px=px, tq=tq: be.activation(out=Ib_[:, tq * 512:(tq + 1) * 512], in_=px[:, :], func=AF.Sigmoid,
                                                                       bias=pvt[:, bxo + c:bxo + c + 1]),
                           reads=[pxb, pvt_b], pwrites=[Ib_b])
                  Mb_ = XR[:, 4:4 + n]
                  K.op(ACT, lambda be: be.activation(out=Ab_[:, 0:n], in_=Rb_[:, 0:n], func=AF.Exp, scale=coef[:, c:c + 1]),
                       reads=[Rb_b, coef_b], writes=[Ab_b])
                  K.op(ACT, lambda be: be.activation(out=Mb_, in_=Rb_[:, 0:n], func=AF.Exp, scale=coef[:, NB + c:NB + c + 1]),
                       reads=[Rb_b, coef_b, Y_b], writes=[XR_b])
                  K.op(ACT, lambda be: be.activation(out=Mb_, in_=Mb_, func=AF.Sqrt, scale=-1.0, bias=1.0),
                       reads=[XR_b], writes=[XR_b])
                  K.op(POOL, lambda be: be.tensor_tensor(out=Ib_[:, 0:n], in0=Ib_[:, 0:n], in1=Y[:, 0:n], op=ALU.mult),
                       reads=[Ib_b, Y_b], writes=[Ib_b])
                  K.op(POOL, lambda be: be.tensor_tensor(out=Ib_[:, 0:n], in0=Ib_[:, 0:n], in1=Mb_, op=ALU.mult),
                       reads=[Ib_b, XR_b], writes=[Ib_b])

              def back(u):
                  c, sgi, s0, n = units[u]
                  st_ = ust.pop(u)
                  (Rb_, Rb_b), (Ab_, Ab_b), (Ib_, Ib_b) = st_["Rb"], st_["Ab"], st_["Ib"]
                  if sgi == 0:
                      K.op(DVE, lambda be: be.tensor_tensor_scan(out=Rb_[:, 0:n], data0=Ab_[:, 0:n], data1=Ib_[:, 0:n], initial=0.0,
                                                                 op0=ALU.mult, op1=ALU.add),
                           reads=[Ab_b, Ib_b], writes=[Rb_b])
                      h0, h0_b = h0_rot.next()
                      h0s[c] = (h0, h0_b)
                      K.op(DVE, lambda be: be.tensor_tensor(out=h0[:, 0:1], in0=Rb_[:, n - 1:n], in1=flag_t[:, 0:1], op=ALU.mult),
                           reads=[Rb_b, flag_b], writes=[h0_b], force=True)
                      return
                  GR, GR_b = st_["GR"]
                  if c in h0s:
                      h0, h0_b = h0s.pop(c)
                      K.op(DVE, lambda be: be.tensor_tensor_scan(out=Hs[:, 0:n], data0=Ab_[:, 0:n], data1=Ib_[:, 0:n], initial=h0[:, 0:1],
                                                                 op0=ALU.mult, op1=ALU.add),
                           reads=[Ab_b, Ib_b, h0_b], writes=[Hs_b])
                  else:
                      K.op(DVE, lambda be: be.tensor_tensor_scan(out=Hs[:, 0:n], data0=Ab_[:, 0:n], data1=Ib_[:, 0:n], initial=0.0,
                                                                 op0=ALU.mult, op1=ALU.add),
                           reads=[Ab_b, Ib_b], writes=[Hs_b])
                  SQ, SQ_b = SQ_rot.next()
                  MO, MO_b = MO_rot.next()
                  K.op(DVE, lambda be: be.tensor_tensor(out=Hs[:, :], in0=GR[:, :], in1=Hs[:, :], op=ALU.mult),
                       reads=[GR_b, Hs_b], writes=[Hs_b])
                  K.op(ACT, lambda be: be.activation(out=SQ[:, :], in_=Hs[:, :], func=AF.Square), reads=[Hs_b], writes=[SQ_b])
                  K.op(DVE, lambda be: be.tensor_scalar(out=MO[:, :], in0=Hs[:, :], scalar1=pvt[:, gRo + c:gRo + c + 1], scalar2=None,
                                                        op0=ALU.mult), reads=[Hs_b, pvt_b], writes=[MO_b])
                  K.dma(QS, mixT_d[H + c, :, :], MO[:, :], reads=[MO_b], writes=[mixT_b[H + c]])
                  NTQ = T // 128
                  for tq in range(NTQ):
                      K.op(PE, lambda be, tq=tq: be.matmul(pss32[:, tq:tq + 1], lhsT=SQ[:, tq * 128:(tq + 1) * 128], rhs=ones[:, 0:1],
                                                           start=True, stop=True), reads=[SQ_b, cbf_b], writes=[pss_b], mark=(tq == NTQ - 1))
                  K.op(DVE, lambda be: be.tensor_tensor(out=ssR[:, :], in0=ssR[:, :], in1=pss32[:, 0:NTQ], op=ALU.add),
                       reads=[pss_b, ss_b], writes=[ss_b])

              NU = len(units)
              front_a1(0)
              front_a2(0)
              front_b(0)
              if NU > 1:
                  front_a1(1)
                  front_a2(1)
              for u in range(NU):
                  cast_tick(3, paced=False)
                  if u + 2 < NU:
                      front_a1(u + 2)
                  if u + 1 < NU:
                      front_b(u + 1)
                  if u + 2 < NU:
                      front_a2(u + 2)
                  back(u)
              K.op(DVE, lambda be: be.tensor_scalar(out=ssA[:, :], in0=ssA[:, :], scalar1=1.0 / DA, scalar2=EPS, op0=ALU.mult, op1=ALU.add),
                   reads=[ss_b], writes=[ss_b])
              K.op(ACT, lambda be: be.activation(out=ssA[:, :], in_=ssA[:, :], func=AF.Ln), reads=[ss_b], writes=[ss_b])
              K.op(ACT, lambda be: be.activation(out=ssA[:, :], in_=ssA[:, :], func=AF.Exp, scale=-0.5), reads=[ss_b], writes=[ss_b])
              K.op(DVE, lambda be: be.tensor_scalar(out=ssR[:, :], in0=ssR[:, :], scalar1=1.0 / DR, scalar2=EPS, op0=ALU.mult, op1=ALU.add),
                   reads=[ss_b], writes=[ss_b])
              K.op(ACT, lambda be: be.activation(out=ssR[:, :], in_=ssR[:, :], func=AF.Ln), reads=[ss_b], writes=[ss_b])
              K.op(ACT, lambda be: be.activation(out=ssR[:, :], in_=ssR[:, :], func=AF.Exp, scale=-0.5), reads=[ss_b], writes=[ss_b])
              if cfg.debug:
                  K.dma(QS, ss_d[:, 0:T // 128], ssA[:, :], reads=[ss_b])
                  K.dma(QS, ss_d[:, T // 128:], ssR[:, :], reads=[ss_b])

        cast_flush()
        with (Stage("C") if "C" in cfg.stages else contextlib.nullcontext()) as st:
          if st is not None:
              ring = make_ring(st)
              rows_rot = st.rot("row", [128, D], F32, 1)
              xn_rot = st.rot("xn", [128, D], BF16, 1)
              sm_rot = st.rot("sm", [128, 4], F32, 4, small=True)
              actT, actT_b = st.tile("actT", [128, KC, 512], BF16)
              FG = cfg.NFFG
              gsz = [(FFC + FG - 1 - g) // FG for g in range(FG)]
              aTf, aT_b = st.tile("aT", [128, max(gsz) * 512], BF16)
              aT = aTf[:, :].rearrange("p (g t) -> p g t", t=512)
              gfin = aTf[:, 0:2 * D].bitcast(F32)
              gfin_b = aT_b
              pc_rot = st.rot("pc", [128, 512], F32, 8)
              sg_rot = st.rot("sg", [128, 512], F32, 3)
              gp_rot = st.rot("gp", [128, 512], F32, 3)
              Wpp, Wpp_b = st.tile("Wpp", [128, PC, D], BF16)
              pT, pT_b = st.tile("pT", [128, PC, 512], BF16)
              prow_rot = st.rot("prow", [128, PLE], F32, 2)
              pbf_rot = st.rot("pbf", [128, PLE], BF16, 2)
              sspe, sspe_b = st.tile("sspe", [128, 8], F32, small=True)
              K.dma(QL, Wpp[:, :, :], wb_pp.rearrange("(k p) n -> p k n", p=128), reads=[wbufs["pp"]], writes=[Wpp_b])

              NO = D // 512
              pe32 = psT[1][0][:, :].bitcast(F32)

              def st_jobs(s_, prev_final=None):
                  t0 = s_ * 512
                  jobs = []
                  pre_pc = {}

                  def load_pc(o, first):
                      for tt in range(4):
                          pc, pcb = pc_rot.next()
                          r0 = t0 + tt * 128
                          if first:
                              K.dma(QL, pc[:, :], x_d[C + r0:C + r0 + 128, o * 512:(o + 1) * 512], writes=[pcb])
                          else:
                              K.dma(QL, pc[:, :], hS_d[r0:r0 + 128, o * 512:(o + 1) * 512], reads=[hb(s_, tt, o)], writes=[pcb])
                          pre_pc[(tt, o)] = (pc, pcb)

                  def mk_pre(first):
                      def pre(o):
                          if o == 0:
                              load_pc(0, first)
                          if o + 1 < NO:
                              load_pc(o + 1, first)
                      return pre

                  def mk_ev_res(ssrc):
                      def ev(tt, o, ps, psb):
                          pc, pcb = pre_pc.pop((tt, o))
                          r0 = t0 + tt * 128
                          idx = s_ * 4 + tt
                          if ssrc is None:
                              K.op(DVE, lambda be: be.tensor_tensor(out=pc[:, :], in0=ps[:, :], in1=pc[:, :], op=ALU.add),
                                   reads=[psb, pcb], writes=[pcb])
                          else:
                              K.op(DVE, lambda be: be.scalar_tensor_tensor(out=pc[:, :], in0=ps[:, :], scalar=ssrc[:, idx:idx + 1],
                                                                           in1=pc[:, :], op0=ALU.mult, op1=ALU.add),
                                   reads=[psb, pcb, ss_b], writes=[pcb])
                          K.dma(QS, hS_d[r0:r0 + 128, o * 512:(o + 1) * 512], pc[:, :], reads=[pcb], writes=[hb(s_, tt, o)])
                      return ev

                  def load_mix(ents, t0=t0):
                      K.dma(QL, actT[:, :, :], mixT_d[:, :, t0:t0 + 512].rearrange("c p t -> p c t"), reads=mixT_b, writes=[actT_b])
                  if s_ == 0:
                      jobs.append(([], load_mix))
                  jobs += gemm_t(wb_out, wbufs["out"], 0, H, 0, D, actT, actT_b, 0, mk_ev_res(ssA), pre=mk_pre(True))
                  jobs += gemm_t(wb_out, wbufs["out"], DA, NB, 0, D, actT, actT_b, H, mk_ev_res(ssR), pre=mk_pre(False))

                  def ffn_norm(ents):
                      for tt in range(4):
                          r0 = t0 + tt * 128
                          norm_transpose(st, hS_d[r0:r0 + 128, :], [hb(s_, tt, o) for o in range(NO)], "gffn", actT, actT_b, tt,
                                         rows_rot, xn_rot, sm_rot)
                  jobs.append(([], ffn_norm))
                  ffn_pos = [len(jobs) + 6]

                  def ev_gu(jg, pss):
                      (pg, pgb), (pu, pub) = pss
                      sg, sgb = sg_rot.next()
                      K.op(ACT, lambda be: be.activation(out=sg[:, :], in_=pg[:, :], func=AF.Silu), reads=[pgb], writes=[sgb])
                      K.op(DVE, lambda be: be.tensor_tensor(out=aT[:, jg, :], in0=sg[:, :], in1=pu[:, :], op=ALU.mult),
                           reads=[sgb, pub], pwrites=[aT_b])
                  f0 = 0
                  for g in range(FG):
                      ng = gsz[g]
                      jobs += gemm_f(wb_gate, wbufs["gate"], KC, f0 * 128, ng * 128, actT, actT_b, ev_gu, pair=(wb_up, wbufs["up"]))
                      jobs += gemm_t(wb_down, wbufs["down"], f0 * 128, ng, 0, D, aT, aT_b, 0, mk_ev_res(None), pre=mk_pre(False))
                      f0 += ng

                  def ple_norm(ents):
                      for tt in range(4):
                          r0 = t0 + tt * 128
                          norm_transpose(st, hS_d[r0:r0 + 128, :], [hb(s_, tt, o) for o in range(NO)], "gple", actT, actT_b, tt,
                                         rows_rot, xn_rot, sm_rot)

                  def ple_prep(ents):
                      for tt in range(4):
                          r0 = t0 + tt * 128
                          prow, prow_b = prow_rot.next()
                          pbf, pbf_b = pbf_rot.next()
                          K.dma(QL, prow[:, :], p_d[r0:r0 + 128, :], writes=[prow_b])
                          K.op(ACT, lambda be, pbf=pbf, prow=prow: be.activation(out=pbf[:, :], in_=prow[:, :], func=AF.Copy),
                               reads=[prow_b], writes=[pbf_b])
                          pt, ptb = psT_rot.next()
                          for j in range(PC):
                              K.op(PE, lambda be, pt=pt, j=j, pbf=pbf: be.transpose(out=pt[:, j * 128:(j + 1) * 128],
                                                                                  in_=pbf[:, j * 128:(j + 1) * 128], identity=ident),
                                   reads=[pbf_b, cbf_b], writes=[ptb], mark=(j == PC - 1))
                          K.op(DVE, lambda be, pt=pt, tt=tt: be.tensor_copy(out=pT[:, :, tt * 128:(tt + 1) * 128],
                                                                            in_=pt[:, 0:PC * 128].rearrange("p (g t) -> p g t", t=128)),
                               reads=[ptb], pwrites=[pT_b])
                      for tt in range(4):
                          for o in range(NO):
                              ps, psb = psF_rot.next()
                              for kc in range(PC):
                                  K.op(PE, lambda be, ps=ps, kc=kc, tt=tt, o=o: be.matmul(
                                      ps[:, :], lhsT=pT[:, kc, tt * 128:(tt + 1) * 128], rhs=Wpp[:, kc, o * 512:(o + 1) * 512],
                                      start=(kc == 0), stop=(kc == PC - 1)), reads=[pT_b, Wpp_b], writes=[psb], mark=(kc == PC - 1))
                              sg, sgb = sg_rot.next()
                              sm, sm_b = sm_rot.next()
                              K.op(ACT, lambda be, sg=sg, ps=ps, sm=sm: be.activation(out=sg[:, :], in_=ps[:, :], func=AF.Square,
                                                                                      accum_out=sm[:, 0:1]),
                                   reads=[psb], writes=[sgb, sm_b])
                              if o == 0:
                                  K.op(DVE, lambda be, sm=sm, tt=tt: be.tensor_copy(out=sspe[:, tt:tt + 1], in_=sm[:, 0:1]),
                                       reads=[sm_b], writes=[sspe_b])
                              else:
                                  K.op(DVE, lambda be, sm=sm, tt=tt: be.tensor_tensor(out=sspe[:, tt:tt + 1], in0=sspe[:, tt:tt + 1],
                                                                                    in1=sm[:, 0:1], op=ALU.add),
                                       reads=[sm_b, sspe_b], writes=[sspe_b])
                      K.op(DVE, lambda be: be.tensor_scalar(out=sspe[:, 4:8], in0=sspe[:, 0:4], scalar1=1.0 / D, scalar2=EPS,
                                                            op0=ALU.mult, op1=ALU.add), reads=[sspe_b], writes=[sspe_b])
                      K.op(ACT, lambda be: be.activation(out=sspe[:, 4:8], in_=sspe[:, 4:8], func=AF.Ln), reads=[sspe_b], writes=[sspe_b])
                      K.op(ACT, lambda be: be.activation(out=sspe[:, 4:8], in_=sspe[:, 4:8], func=AF.Exp, scale=-0.5),
                           reads=[sspe_b], writes=[sspe_b])
                  jobs.insert(ffn_pos[0], ([], ple_prep))
                  jobs.append(([], ple_norm))
                  gp_d = {}

                  def pre_ple(o):
                      def ld(o2):
                          gp, gpb = gp_rot.next()
                          K.dma(QL, gp[:, :], gpo_d[0:1, o2 * 512:(o2 + 1) * 512].partition_broadcast(128), writes=[gpb])
                          gp_d[o2] = (gp, gpb)
                          load_pc(o2, False)
                      if o == 0:
                          ld(0)
                      if o + 1 < NO:
                          ld(o + 1)

                  def ev_ple(tt, o, ps, psb):
                      r0 = t0 + tt * 128
                      gp, gpb = gp_d[o]
                      sg, sgb = sg_rot.next()
                      K.op(ACT, lambda be: be.activation(out=sg[:, :], in_=ps[:, :], func=AF.Sigmoid), reads=[psb], writes=[sgb])
                      pe_, pe_b = pe32, psT[1][1]
                      for kc in range(PC):
                          K.op(PE, lambda be, kc=kc: be.matmul(pe_[:, :], lhsT=pT[:, kc, tt * 128:(tt + 1) * 128],
                                                               rhs=Wpp[:, kc, o * 512:(o + 1) * 512], start=(kc == 0), stop=(kc == PC - 1)),
                               reads=[pT_b, Wpp_b], writes=[pe_b], mark=(kc == PC - 1))
                      sg2, sg2b = sg_rot.next()
                      K.op(DVE, lambda be: be.scalar_tensor_tensor(out=sg2[:, :], in0=pe_[:, :], scalar=sspe[:, 4 + tt:5 + tt], in1=gp[:, :],
                                                                   op0=ALU.mult, op1=ALU.mult), reads=[pe_b, sspe_b, gpb], writes=[sg2b])
                      K.op(DVE, lambda be: be.tensor_tensor(out=sg2[:, :], in0=sg2[:, :], in1=sg[:, :], op=ALU.mult),
                           reads=[sgb, sg2b], writes=[sg2b])
                      pc, pcb = pre_pc.pop((tt, o))
                      K.op(DVE, lambda be: be.tensor_tensor(out=pc[:, :], in0=pc[:, :], in1=sg2[:, :], op=ALU.add),
                           reads=[sg2b, pcb], writes=[pcb])
                      K.dma(QS, hS_d[r0:r0 + 128, o * 512:(o + 1) * 512], pc[:, :], reads=[pcb], writes=[hb(s_, tt, o)])
                  jobs += gemm_t(wb_pg, wbufs["pg"], 0, KC, 0, D, actT, actT_b, 0, ev_ple, pre=pre_ple)

                  def final_norm(ents):
                      for tt in range(4):
                          r0 = t0 + tt * 128
                          hrow, hrow_b = rows_rot.next()
                          xn, xn_b = xn_rot.next()
                          sm, sm_b = sm_rot.next()
                          K.dma(QL, hrow[:, :], hS_d[r0:r0 + 128, :], reads=[hb(s_, tt, o) for o in range(NO)], writes=[hrow_b])
                          K.op(ACT, lambda be, xn=xn, hrow=hrow, sm=sm: be.activation(out=xn[:, :], in_=hrow[:, :], func=AF.Square,
                                                                                      accum_out=sm[:, 0:1]),
                               reads=[hrow_b], writes=[xn_b, sm_b])
                          K.op(DVE, lambda be, sm=sm: be.tensor_scalar(out=sm[:, 1:2], in0=sm[:, 0:1], scalar1=1.0 / D, scalar2=EPS,
                                                                       op0=ALU.mult, op1=ALU.add), reads=[sm_b], writes=[sm_b])
                          K.op(ACT, lambda be, sm=sm: be.activation(out=sm[:, 2:3], in_=sm[:, 1:2], func=AF.Ln), reads=[sm_b], writes=[sm_b])
                          K.op(ACT, lambda be, sm=sm: be.activation(out=sm[:, 2:3], in_=sm[:, 2:3], func=AF.Exp, scale=-0.5),
                               reads=[sm_b], writes=[sm_b])
                          for o in range(NO):
                              gp, gpb = gp_rot.next()
                              K.dma(QL, gp[:, :], gfin_d[0:1, o * 512:(o + 1) * 512].partition_broadcast(128), writes=[gpb])
                              K.op(DVE, lambda be, hrow=hrow, sm=sm, gp=gp, o=o: be.scalar_tensor_tensor(
                                  out=hrow[:, o * 512:(o + 1) * 512], in0=hrow[:, o * 512:(o + 1) * 512], scalar=sm[:, 2:3],
                                  in1=gp[:, :], op0=ALU.mult, op1=ALU.mult),
                                  reads=[hrow_b, sm_b, gpb], writes=[hrow_b])
                          K.dma(QS, out_d[r0:r0 + 128, :], hrow[:, :], reads=[hrow_b])
                  if s_ + 1 < cfg.NST:
                      jobs.append(([], lambda ents: K.dma(QL, actT[:, :, :], mixT_d[:, :, t0 + 512:t0 + 1024].rearrange("c p t -> p c t"),
                                                          reads=mixT_b, writes=[actT_b])))
                  if prev_final is not None:
                      jobs.insert(ffn_pos[0] - 3, ([], prev_final))
                  if s_ + 1 == cfg.NST:
                      jobs.append(([], final_norm))
                  return jobs, final_norm

              alljobs = []
              pf = None
              for s_ in range(cfg.NST):
                  j_, pf = st_jobs(s_, pf)
                  alljobs += j_
              stream(ring, alljobs)

        K.finish()

        with nc.Block() as block:
            @block.tensor
            def _(be):
                for f in PE.prog:
                    f(be)

            @block.scalar
            def _(be):
                for f in ACT.prog:
                    f(be)

            @block.vector
            def _(be):
                for f in DVE.prog:
                    f(be)

            @block.gpsimd
            def _(be):
                for f in POOL.prog:
                    f(be)

            @block.sync
            def _(be):
                for f in SP.prog:
                    f(be)
        nc._k_stats = (K.nops, K.ndma)
    return nc


def make_consts(cfg):
    c = np.zeros((128, cfg.NCONST), np.float32)
    c[:, 0:128] = np.eye(128, dtype=np.float32)
    j = np.arange(128)[:, None]
    s = np.arange(128)[None, :]
    c[:, 128:256] = (j >= s).astype(np.float32)
    c[:, 256:384] = 1.0
    col = np.arange(512)[None, :]
    for r in range(4):
        c[:, 384 + r * 512:384 + (r + 1) * 512] = ((r * 128 + j) < col).astype(np.float32)
    return c


def colmajor(v, n):
    return np.ascontiguousarray(np.asarray(v, np.float32).reshape(n, 128).T)


def make_pv(cfg, g_mix, g_ffn, g_ple, conv_w, conv_b, b_rg_a, b_rg_x, rg_lambda, g_attn_out, g_rnn_out):
    pv = np.zeros((128, cfg.NPV), np.float32)

    def put(name, arr):
        o, n = cfg.pv[name]
        pv[:, o:o + n] = arr
    put("gmix", colmajor(g_mix, cfg.KC))
    put("gffn", colmajor(g_ffn, cfg.KC))
    put("gple", colmajor(g_ple, cfg.KC))
    cw = np.asarray(conv_w, np.float32)
    cwp = np.stack([colmajor(cw[i], cfg.NB) for i in range(4)], axis=2)
    put("cw", cwp.reshape(128, cfg.NB * 4))
    put("cb", colmajor(conv_b, cfg.NB))
    put("ba", colmajor(b_rg_a, cfg.NB))
    put("bx", colmajor(b_rg_x, cfg.NB))
    put("lam", colmajor(rg_lambda, cfg.NB))
    put("gA", colmajor(g_attn_out, cfg.H))
    put("gR", colmajor(g_rnn_out, cfg.NB))
    return pv


def make_in_maps(cfg, nbatch, x, p, g_mix, w_in, conv_w, conv_b, w_rg_a, b_rg_a, w_rg_x, b_rg_x, rg_lambda, g_attn_out,
                 g_rnn_out, w_out, g_ffn, w_ffn_gate, w_ffn_up, w_ffn_down, g_ple, w_ple_gate, w_ple_proj, g_ple_out,
                 g_final):
    f = lambda a: np.ascontiguousarray(np.asarray(a, np.float32))
    shared = {
        "pv": make_pv(cfg, g_mix[0], g_ffn[0], g_ple[0], conv_w[0], conv_b[0], b_rg_a[0], b_rg_x[0], rg_lambda[0],
                      g_attn_out[0], g_rnn_out[0]),
        "consts": make_consts(cfg),
        "gpo": f(g_ple_out[0]).reshape(1, cfg.D),
        "gfin": f(g_final).reshape(1, cfg.D),
        "w_in": f(w_in[0]), "w_out": f(w_out[0]), "w_gate": f(w_ffn_gate[0]), "w_up": f(w_ffn_up[0]),
        "w_down": f(w_ffn_down[0]), "w_pg": f(w_ple_gate[0]), "w_pp": f(w_ple_proj[0]),
        "w_ra": f(w_rg_a[0]).reshape(cfg.NB * 128, 128), "w_rx": f(w_rg_x[0]).reshape(cfg.NB * 128, 128),
    }
    x = np.asarray(x, np.float32)
    p = np.asarray(p, np.float32)
    in_maps = []
    for b in range(nbatch):
        for half in range(2):
            m = dict(shared)
            if half == 0:
                xc = np.concatenate([np.zeros((cfg.C, cfg.D), np.float32), x[b, 0:cfg.T]], axis=0)
            else:
                xc = np.ascontiguousarray(x[b, 0:cfg.C + cfg.T])
            m["x"] = xc
            m["p"] = np.ascontiguousarray(p[0, b, half * cfg.T:(half + 1) * cfg.T])
            m["flag"] = np.full((128, 1), float(half), np.float32)
            in_maps.append(m)
    return in_maps


def kernel(**inputs):
    cfg = Cfg()
    nc = build_nc(cfg)
    in_maps = make_in_maps(cfg, 4, **inputs)
    res = run_bass_kernel_spmd(nc, in_maps, core_ids=list(range(8)))
    out = np.empty((4, 4096, 4096), np.float32)
    for b in range(4):
        for half in range(2):
            out[b, half * 2048:(half + 1) * 2048] = res.results[b * 2 + half]["out"]
    return out
```

```python
import contextlib
import numpy as np
import concourse.bass as bass
import concourse.mybir as mybir
from concourse.bass_utils import run_bass_kernel_spmd

F32 = mybir.dt.float32
BF16 = mybir.dt.bfloat16
AF = mybir.ActivationFunctionType
ALU = mybir.AluOpType
EPS = 1e-6
SAME_SYNC = True
FULL_SELF_SYNC = False


class Cfg:
    def __init__(self, D=4096, T=2048, C=2048, DFF=11008, PLE=256, NFFG=4, debug=False, stages="0cABRC"):
        self.stages = stages
        self.D, self.T, self.C, self.DFF, self.PLE, self.NFFG, self.debug = D, T, C, DFF, PLE, NFFG, debug
        self.DA = D // 2
        self.DR = D // 2
        self.H = self.DA // 128
        self.NB = self.DR // 128
        self.KC = D // 128
        self.DIN = 3 * self.DA + 2 * self.DR
        self.NTOK = C + T
        self.NBLK = self.NTOK // 128
        self.NST = T // 512
        self.NSTA = self.NTOK // 512
        self.FFC = DFF // 128
        self.PC = PLE // 128
        o = 0
        self.pv = {}
        for name, n in [("gmix", self.KC), ("gffn", self.KC), ("gple", self.KC), ("cw", self.NB * 4),
                        ("cb", self.NB), ("ba", self.NB), ("bx", self.NB), ("lam", self.NB),
                        ("gA", self.H), ("gR", self.NB)]:
            self.pv[name] = (o, n)
            o += n
        self.NPV = o
        self.NCONST = 128 * 3 + 4 * 512


class Tok:
    __slots__ = ("sem", "val", "sid")

    def __init__(self, sem, val, sid):
        self.sem, self.val, self.sid = sem, val, sid


class Buf:
    __slots__ = ("name", "w", "r", "pend", "excl", "small")

    def __init__(self, name="", excl=False, small=False):
        self.name = name
        self.excl = excl
        self.small = small
        self.w = {}
        self.r = {}
        self.pend = None


class Eng:
    def __init__(self, K, name, selfsync):
        self.K = K
        self.name = name
        self.sem = K.new_sem("e_" + name)
        self.sid = K.new_sid()
        self.cnt = 0
        self.seen = {}
        self.pending = []
        self.prog = []
        self.selfsync = selfsync

    def wait(self, tok, force=False):
        if tok.sid == self.sid and not (self.selfsync and (force or FULL_SELF_SYNC)):
            return
        if self.seen.get(tok.sid, 0) >= tok.val:
            return
        self.seen[tok.sid] = tok.val
        sem, val = tok.sem, tok.val
        self.prog.append(lambda be: be.wait_ge(sem, val))


class DmaQ:
    def __init__(self, K, eng, nslots, name):
        self.eng = eng
        self.slots = []
        for i in range(nslots):
            self.slots.append([K.new_sem("d_%s%d" % (name, i)), 0, K.new_sid()])
        self.i = 0


class Tracker:
    def __init__(self, nc, stack):
        self.nc = nc
        self.stack = stack
        self._sid = 0
        self.pe = Eng(self, "pe", False)
        self.act = Eng(self, "act", SAME_SYNC)
        self.dve = Eng(self, "dve", SAME_SYNC)
        self.pool = Eng(self, "pool", SAME_SYNC)
        self.sp = Eng(self, "sp", False)
        self.engs = [self.pe, self.act, self.dve, self.pool, self.sp]
        self.qld = DmaQ(self, self.sp, 24, "l")
        self.qst = DmaQ(self, self.sp, 16, "s")
        self.qcast = DmaQ(self, self.pool, 3, "c")
        self.ndma = 0
        self.nops = 0

    def new_sem(self, name):
        return self.stack.enter_context(self.nc.semaphore(name))

    def new_sid(self):
        self._sid += 1
        return self._sid

    def _deps(self, eng, reads, writes, pwrites, force=False):
        for b in reads:
            if b.pend is not None and b.pend is not eng:
                raise RuntimeError("buffer %s has pending unflushed access" % b.name)
            f = force or b.small
            for t in list(b.w.values()):
                eng.wait(t, f)
            if b.excl:
                for t in list(b.r.values()):
                    if t.sid != eng.sid:
                        eng.wait(t)
        for b in writes:
            if b.pend is not None and b.pend is not eng:
                raise RuntimeError("buffer %s has pending unflushed access" % b.name)
            f = force or b.small
            for t in list(b.w.values()):
                eng.wait(t, f)
            for t in list(b.r.values()):
                eng.wait(t, f)
        for b in pwrites:
            if b.pend is not None and b.pend is not eng:
                raise RuntimeError("buffer %s has pending unflushed access" % b.name)
            f = force or b.small
            for t in list(b.r.values()):
                eng.wait(t, f)

    def op(self, eng, fn, reads=(), writes=(), pwrites=(), mark=True, force=False):
        self._deps(eng, reads, writes, pwrites, force)
        self.nops += 1
        eng.pending.append((reads, writes, pwrites))
        for b in reads:
            b.pend = eng
        for b in writes:
            b.pend = eng
        for b in pwrites:
            b.pend = eng
        if mark:
            eng.cnt += 1
            sem = eng.sem
            eng.prog.append(lambda be: fn(be).then_inc(sem, 1))
            tok = Tok(eng.sem, eng.cnt, eng.sid)
            for (r, w, pw) in eng.pending:
                for b in r:
                    b.r[eng.sid] = tok
                    b.pend = None
                for b in w:
                    b.w = {eng.sid: tok}
                    b.r = {}
                    b.pend = None
                for b in pw:
                    b.w[eng.sid] = tok
                    b.pend = None
            eng.pending = []
        else:
            eng.prog.append(lambda be: fn(be))

    def dma(self, q, out, in_, reads=(), writes=(), pwrites=(), **kw):
        eng = q.eng
        assert not eng.pending
        self._deps(eng, reads, writes, pwrites)
        slot = q.slots[q.i % len(q.slots)]
        q.i += 1
        self.ndma += 1
        sem, val, sid = slot
        if val > 0:
            eng.wait(Tok(sem, val, sid))
        slot[1] = val + 16
        eng.prog.append(lambda be: be.dma_start(out=out, in_=in_, **kw).then_inc(sem, 16))
        tok = Tok(sem, val + 16, sid)
        for b in reads:
            b.r[sid] = tok
        for b in writes:
            b.w = {sid: tok}
            b.r = {}
        for b in pwrites:
            b.w[sid] = tok

    def barrier(self):
        toks = []
        for e in self.engs:
            assert not e.pending, e.name
            if e.cnt:
                toks.append(Tok(e.sem, e.cnt, e.sid))
        for q in (self.qld, self.qst, self.qcast):
            for sem, val, sid in q.slots:
                if val:
                    toks.append(Tok(sem, val, sid))
        for e in self.engs:
            for t in toks:
                if t.sid != e.sid:
                    e.wait(t)

    def finish(self):
        self.barrier()


class Rot:
    def __init__(self, items):
        self.items = items
        self.i = 0

    def next(self):
        it = self.items[self.i % len(self.items)]
        self.i += 1
        return it


def build_nc(cfg):
    nc = bass.Bass("TRN2", target_bir_lowering=False)
    D, T, C, DFF, PLE = cfg.D, cfg.T, cfg.C, cfg.DFF, cfg.PLE
    DA, DR, H, NB, KC, DIN, NTOK, NBLK = cfg.DA, cfg.DR, cfg.H, cfg.NB, cfg.KC, cfg.DIN, cfg.NTOK, cfg.NBLK
    FFC, PC = cfg.FFC, cfg.PC

    def din(name, shape, dt=F32):
        return nc.dram_tensor(name, list(shape), dt, kind="ExternalInput").ap()

    dbg_kind = "ExternalOutput" if cfg.debug else "Internal"

    def dscr(name, shape, dt, dbg=False):
        return nc.dram_tensor(name, list(shape), dt, kind=(dbg_kind if dbg else "Internal")).ap()

    x_d = din("x", [NTOK, D])
    p_d = din("p", [T, PLE])
    flag_d = din("flag", [128, 1])
    pv_d = din("pv", [128, cfg.NPV])
    const_d = din("consts", [128, cfg.NCONST])
    gpo_d = din("gpo", [1, D])
    gfin_d = din("gfin", [1, D])
    w_in_d = din("w_in", [D, DIN])
    w_out_d = din("w_out", [D, D])
    w_gate_d = din("w_gate", [D, DFF])
    w_up_d = din("w_up", [D, DFF])
    w_down_d = din("w_down", [DFF, D])
    w_pg_d = din("w_pg", [D, D])
    w_pp_d = din("w_pp", [PLE, D])
    w_ra_d = din("w_ra", [NB * 128, 128])
    w_rx_d = din("w_rx", [NB * 128, 128])
    out_d = nc.dram_tensor("out", [T, D], F32, kind="ExternalOutput").ap()

    wb_in = dscr("wb_in", [D, DIN], BF16)
    wb_out = dscr("wb_out", [D, D], BF16)
    wb_gate = dscr("wb_gate", [D, DFF], BF16)
    wb_up = dscr("wb_up", [D, DFF], BF16)
    wb_down = dscr("wb_down", [DFF, D], BF16)
    wb_pg = dscr("wb_pg", [D, D], BF16)
    wb_pp = dscr("wb_pp", [PLE, D], BF16)
    wb_ra = dscr("wb_ra", [NB * 128, 128], BF16)
    wb_rx = dscr("wb_rx", [NB * 128, 128], BF16)
    qT_d = dscr("qT", [H, 128, T], BF16, True)
    kT_d = dscr("kT", [H, 128, NTOK], BF16, True)
    vS_d = dscr("vS", [H, 128, NBLK, 128], BF16, True)
    xrT_d = dscr("xrT", [NB, 128, NTOK], F32, True)
    grT_d = dscr("grT", [NB, 128, T], F32, True)
    mixT_d = dscr("mixT", [H + NB, 128, T], BF16, True)
    ss_d = dscr("ssdbg", [128, 2 * (T // 128)], F32, True)
    hS_d = dscr("hS", [T, D], F32, True)

    with contextlib.ExitStack() as gstack:
        K = Tracker(nc, gstack)
        PE, ACT, DVE, POOL, SP = K.pe, K.act, K.dve, K.pool, K.sp
        QL, QS, QC = K.qld, K.qst, K.qcast

        def sb(stack, name, shape, dt):
            return stack.enter_context(nc.sbuf_tensor("s_" + name, list(shape), dt))

        psT = []
        for i in range(2):
            t = gstack.enter_context(nc.psum_tensor("psT%d" % i, [128, 1024], BF16))
            psT.append((t, Buf("psT%d" % i, excl=True)))
        psF = []
        psBig = []
        for i in range(3):
            t = gstack.enter_context(nc.psum_tensor("psB%d" % i, [128, 1024], F32))
            psBig.append(t)
            for h_ in range(2):
                psF.append((t[:, h_ * 512:(h_ + 1) * 512], Buf("psF%d" % (2 * i + h_), excl=True)))
        psT_rot = Rot(psT)
        psF_rot = Rot(psF)

        cst32 = sb(gstack, "cst32", [128, cfg.NCONST], F32)
        cst32_b = Buf("cst32")
        cbf = sb(gstack, "cbf", [128, 384], BF16)
        cbf_b = Buf("cbf")
        pvt = sb(gstack, "pvt", [128, cfg.NPV], F32)
        pvt_b = Buf("pvt")
        flag_t = sb(gstack, "flag", [128, 1], F32)
        flag_b = Buf("flag")
        ssA = sb(gstack, "ssA", [128, T // 128], F32)
        ssR = sb(gstack, "ssR", [128, T // 128], F32)
        ss_b = Buf("ss", small=True)
        coef = sb(gstack, "coef", [128, 2 * NB], F32)
        coef_b = Buf("coef", small=True)

        K.dma(QL, cst32[:, :], const_d[:, :], writes=[cst32_b])
        K.dma(QL, pvt[:, :], pv_d[:, :], writes=[pvt_b])
        K.dma(QL, flag_t[:, :], flag_d[:, :], writes=[flag_b])
        K.op(DVE, lambda be: be.tensor_copy(out=cbf[:, :], in_=cst32[:, 0:384]), reads=[cst32_b], writes=[cbf_b])
        K.op(DVE, lambda be: be.memset(ssA[:, :], 0.0), writes=[ss_b])
        K.op(DVE, lambda be: be.memset(ssR[:, :], 0.0), pwrites=[ss_b])
        ident = cbf[:, 0:128]
        Lmat = cbf[:, 128:256]
        ones = cbf[:, 256:384]

        def maskM(r):
            return cst32[:, 384 + r * 512: 384 + (r + 1) * 512]

        def pv(name, j0=0, n=None):
            o, nn = cfg.pv[name]
            if n is None:
                n = nn - j0
            return pvt[:, o + j0: o + j0 + n]

        lo, ln_ = cfg.pv["lam"]
        if 'c' not in cfg.stages:
            K_op_real = K.op
            K.op = lambda *a, **k: None
        K.op(ACT, lambda be: be.activation(out=coef[:, 0:NB], in_=pvt[:, lo:lo + NB], func=AF.Exp, scale=-1.0),
             reads=[pvt_b], writes=[coef_b])
        K.op(ACT, lambda be: be.activation(out=coef[:, 0:NB], in_=coef[:, 0:NB], func=AF.Ln, bias=1.0),
             reads=[coef_b], writes=[coef_b])
        K.op(DVE, lambda be: be.tensor_scalar(out=coef[:, NB:2 * NB], in0=coef[:, 0:NB], scalar1=-16.0, scalar2=None,
                                              op0=ALU.mult), reads=[coef_b], pwrites=[coef_b])
        K.op(DVE, lambda be: be.tensor_scalar(out=coef[:, 0:NB], in0=coef[:, 0:NB], scalar1=-8.0, scalar2=None,
                                              op0=ALU.mult), reads=[coef_b], writes=[coef_b])

        if 'c' not in cfg.stages:
            K.op = K_op_real
        wbufs = {}
        cast_list = []
        cast_ptr = [0]
        b_tot = {}
        b_done = {}

        def cast_tick(n, paced=True):
            for _ in range(n):
                if cast_ptr[0] >= len(cast_list):
                    return
                d, s_, b = cast_list[cast_ptr[0]]
                cast_ptr[0] += 1
                if paced and PE.cnt > 0:
                    POOL.wait(Tok(PE.sem, PE.cnt, PE.sid))
                K.dma(QC, d, s_, pwrites=[b])
                b_done[id(b)] = b_done.get(id(b), 0) + 1

        def cast_ready(b):
            return b_done.get(id(b), 0) == b_tot.get(id(b), 0)

        def cast_flush():
            cast_tick(len(cast_list), paced=True)

        def cast_weight(name, src, dst, rows, cols, c0=0, ncol=None):
            b = Buf("wb_" + name)
            wbufs[name] = b
            if ncol is None:
                ncol = cols
            cw = ncol
            while cw > 2048:
                for dv in (2, 3, 5, 7, 43):
                    if cw % dv == 0:
                        cw //= dv
                        break
                else:
                    raise ValueError
            rstep = max(1, min(rows, (2 << 20) // (ncol * 4)))
            for r0 in range(0, rows, rstep):
                r1 = min(rows, r0 + rstep)
                s = src[r0:r1, c0:c0 + ncol].rearrange("r (a b) -> r a b", b=cw)
                d = dst[r0:r1, c0:c0 + ncol].rearrange("r (a b) -> r a b", b=cw)
                cast_list.append((d, s, b))
                b_tot[id(b)] = b_tot.get(id(b), 0) + 1

        if "0" not in cfg.stages:
            cast_weight = lambda *a, **k: None
        cast_weight("in_k", w_in_d, wb_in, D, DIN, DA, DA)
        cast_weight("in_v", w_in_d, wb_in, D, DIN, 2 * DA, DA)
        cast_weight("in_xr", w_in_d, wb_in, D, DIN, 3 * DA, DR)
        cast_weight("in_q", w_in_d, wb_in, D, DIN, 0, DA)
        cast_weight("in_gr", w_in_d, wb_in, D, DIN, 3 * DA + DR, DR)
        cast_weight("ra", w_ra_d, wb_ra, NB * 128, 128)
        cast_weight("rx", w_rx_d, wb_rx, NB * 128, 128)
        cast_weight("out", w_out_d, wb_out, D, D)
        cast_weight("gate", w_gate_d, wb_gate, D, DFF)
        cast_weight("up", w_up_d, wb_up, D, DFF)
        cast_weight("down", w_down_d, wb_down, DFF, D)
        cast_weight("pg", w_pg_d, wb_pg, D, D)
        cast_weight("pp", w_pp_d, wb_pp, PLE, D)
        n0 = sum(b_tot[id(wbufs[k_])] for k_ in ("in_k",)) if "0" in cfg.stages else 0
        cast_tick(n0, paced=False)

        RN = 4
        evac_flip = [0]

        class Stage:
            def __init__(self, name, need_rows=True):
                self.stack = contextlib.ExitStack()
                self.name = name

            def __enter__(self):
                self.stack.__enter__()
                return self

            def __exit__(self, *a):
                K.barrier()
                return self.stack.__exit__(*a)

            def tile(self, name, shape, dt, small=False):
                return sb(self.stack, self.name + "_" + name, shape, dt), Buf(self.name + "_" + name, small=small)

            def rot(self, name, shape, dt, n, small=False):
                return Rot([self.tile("%s%d" % (name, i), shape, dt, small) for i in range(n)])

        def make_ring(st):
            return [st.tile("ring%d" % i, [128, 8192], BF16) for i in range(RN)]

        ring_ctr = [0]
        cast_every = [12]

        def cast_flush_buf(b):
            while not cast_ready(b):
                cast_tick(1)

        def stream(ring, jobs):
            flat = []
            starts = []
            for slabs, _ in jobs:
                starts.append(len(flat))
                flat.extend(slabs)
            base = ring_ctr[0]
            ring_ctr[0] += len(flat)
            issued = 0
            ents = {}
            for ji, (slabs, compute) in enumerate(jobs):
                lim = min(len(flat), starts[ji] + RN)
                while issued < lim:
                    src, shape, wbuf = flat[issued]
                    tl, bf = ring[(base + issued) % RN]
                    n = 1
                    for s_ in shape[1:]:
                        n *= s_
                    if len(shape) == 3:
                        view = tl[:, 0:n].rearrange("p (k n) -> p k n", n=shape[2])
                    else:
                        view = tl[:, 0:n]
                    if not cast_ready(wbuf):
                        cast_flush_buf(wbuf)
                    K.dma(QL, view, src, reads=[wbuf], writes=[bf])
                    ents[issued] = (view, bf)
                    issued += 1
                if ji % cast_every[0] == 0:
                    cast_tick(1)
                compute([ents[starts[ji] + i] for i in range(len(slabs))])
                for i in range(len(slabs)):
                    del ents[starts[ji] + i]

        def gemm_f(wb, wbuf, Kc, col0, ncols, actT, actT_b, evac, pair=None):
            wv = wb.rearrange("(k p) n -> p k n", p=128)
            jobs = []
            c = col0
            jg0 = 0
            while c < col0 + ncols:
                w_ = min(256, col0 + ncols - c)
                slabs = [(wv[:, 0:Kc, c:c + w_], [128, Kc, w_], wbuf)]
                if pair is not None:
                    wv2 = pair[0].rearrange("(k p) n -> p k n", p=128)
                    slabs.append((wv2[:, 0:Kc, c:c + w_], [128, Kc, w_], pair[1]))

                def compute(entries, jg0=jg0, w_=w_):
                    for j in range(w_ // 128):
                        pss = []
                        for (view, bf) in entries:
                            ps, psb = psF_rot.next()
                            for kc in range(Kc):
                                K.op(PE, lambda be, ps=ps, view=view, kc=kc, j=j: be.matmul(
                                    ps[:, :], lhsT=view[:, kc, j * 128:(j + 1) * 128], rhs=actT[:, kc, :],
                                    start=(kc == 0), stop=(kc == Kc - 1)),
                                    reads=[bf, actT_b], writes=[psb], mark=(kc == Kc - 1))
                            pss.append((ps, psb))
                        evac(jg0 + j, pss)
                jobs.append((slabs, compute))
                jg0 += w_ // 128
                c += w_
            return jobs

        def gemm_t(wb, wbuf, row0, nK, col0, ncols, actT, actT_b, kc_off, evac, pre=None):
            pieces = []
            k0 = 0
            while k0 < nK:
                n = min(16, nK - k0)
                pieces.append((k0, n))
                k0 += n
            jobs = []
            NO = ncols // 512
            bankd = {}
            for o in range(NO):
                for pi, (k0, n) in enumerate(pieces):
                    r0 = row0 + k0 * 128
                    src = wb[r0:r0 + n * 128, col0 + o * 512: col0 + (o + 1) * 512].rearrange("(k p) n -> p k n", p=128)
                    slabs = [(src, [128, n, 512], wbuf)]

                    def compute(entries, o=o, pi=pi, k0=k0, n=n):
                        view, bf = entries[0]
                        if pi == 0:
                            bankd[o] = [psF_rot.next() for _ in range(4)]
                            if pre is not None:
                                pre(o)
                        banks = bankd[o]
                        for tt in range(4):
                            ps, psb = banks[tt]
                            for kl in range(n):
                                K.op(PE, lambda be, ps=ps, view=view, kl=kl, tt=tt, k0=k0, pi=pi, n=n: be.matmul(
                                    ps[:, :], lhsT=actT[:, kc_off + k0 + kl, tt * 128:(tt + 1) * 128], rhs=view[:, kl, :],
                                    start=(pi == 0 and kl == 0), stop=(pi == len(pieces) - 1 and kl == n - 1)),
                                    reads=[bf, actT_b], writes=[psb], mark=(kl == n - 1))
                            if pi == len(pieces) - 1:
                                evac(tt, o, ps, psb)
                    jobs.append((slabs, compute))
            return jobs

        def norm_transpose(st, src_rows, src_bufs, gname, actT, actT_b, tt, rows_rot, xn_rot, sm_rot):
            hrow, hrow_b = rows_rot.next()
            xn, xn_b = xn_rot.next()
            sm, sm_b = sm_rot.next()
            K.dma(QL, hrow[:, :], src_rows, reads=src_bufs, writes=[hrow_b])
            K.op(ACT, lambda be: be.activation(out=xn[:, :], in_=hrow[:, :], func=AF.Square, accum_out=sm[:, 0:1]),
                 reads=[hrow_b], writes=[xn_b, sm_b])
            K.op(DVE, lambda be: be.tensor_scalar(out=sm[:, 1:2], in0=sm[:, 0:1], scalar1=1.0 / D, scalar2=EPS,
                                                  op0=ALU.mult, op1=ALU.add), reads=[sm_b], writes=[sm_b])
            K.op(ACT, lambda be: be.activation(out=sm[:, 2:3], in_=sm[:, 1:2], func=AF.Ln), reads=[sm_b], writes=[sm_b])
            K.op(ACT, lambda be: be.activation(out=sm[:, 2:3], in_=sm[:, 2:3], func=AF.Exp, scale=-0.5), reads=[sm_b], writes=[sm_b])
            K.op(DVE, lambda be: be.tensor_scalar(out=xn[:, :], in0=hrow[:, :], scalar1=sm[:, 2:3], scalar2=None,
                                                  op0=ALU.mult), reads=[hrow_b, sm_b], writes=[xn_b])
            G = min(8, KC)
            go, _ = cfg.pv[gname]
            for g0 in range(0, KC, G):
                pt, ptb = psT_rot.next()
                for j in range(G):
                    kc = g0 + j
                    K.op(PE, lambda be, pt=pt, j=j, kc=kc: be.transpose(
                        out=pt[:, j * 128:(j + 1) * 128], in_=xn[:, kc * 128:(kc + 1) * 128], identity=ident),
                        reads=[xn_b, cbf_b], writes=[ptb], mark=(j == G - 1))
                gv = pvt[:, go + g0: go + g0 + G].unsqueeze(2).to_broadcast([128, G, 128])
                K.op(DVE, lambda be, pt=pt, g0=g0, gv=gv: be.tensor_tensor(
                    out=actT[:, g0:g0 + G, tt * 128:(tt + 1) * 128],
                    in0=pt[:, 0:G * 128].rearrange("p (g t) -> p g t", t=128), in1=gv, op=ALU.mult),
                    reads=[ptb, pvt_b], pwrites=[actT_b])

        def evac_copy(dst, ps, reads, writes=(), pwrites=()):
            evac_flip[0] ^= 1
            if evac_flip[0]:
                K.op(ACT, lambda be: be.activation(out=dst, in_=ps, func=AF.Copy), reads=reads, writes=writes, pwrites=pwrites)
            else:
                K.op(DVE, lambda be: be.tensor_copy(out=dst, in_=ps), reads=reads, writes=writes, pwrites=pwrites)

        qT_b = [Buf("qT%d" % h) for h in range(H)]
        kT_b = [Buf("kT%d" % h) for h in range(H)]
        vS_b = [Buf("vS%d" % h) for h in range(H)]
        xrT_b = [Buf("xrT%d" % c) for c in range(NB)]
        grT_b = [Buf("grT%d" % c) for c in range(NB)]
        mixT_b = [Buf("mixT%d" % c) for c in range(H + NB)]
        hS_b = {}

        def hb(st_, tt, o):
            k_ = (st_, tt, o)
            if k_ not in hS_b:
                hS_b[k_] = Buf("hS%d_%d_%d" % k_)
            return hS_b[k_]

        with (Stage("A") if "A" in cfg.stages else contextlib.nullcontext()) as st:
          if st is not None:
              ring = make_ring(st)
              rows_rot = st.rot("row", [128, D], F32, 2)
              xn_rot = st.rot("xn", [128, D], BF16, 2)
              sm_rot = st.rot("sm", [128, 4], F32, 4, small=True)
              uT_rot = st.rot("uT", [128, KC, 512], BF16, 2)
              ob_rot = st.rot("ob", [128, 512], BF16, 4)
              of_rot = st.rot("of", [128, 512], F32, 4)
              uTs = [uT_rot.next() for _ in range(cfg.NSTA)]

              def mk_evs(g0, t0):
                  def ev_q(jg, pss):
                      (ps, psb), = pss
                      ob, obb = ob_rot.next()
                      evac_copy(ob[:, :], ps[:, :], [psb], writes=[obb])
                      K.dma(QS, qT_d[jg, :, t0:t0 + 512], ob[:, :], reads=[obb], pwrites=[qT_b[jg]])

                  def ev_k(jg, pss):
                      (ps, psb), = pss
                      ob, obb = ob_rot.next()
                      evac_copy(ob[:, :], ps[:, :], [psb], writes=[obb])
                      K.dma(QS, kT_d[jg, :, g0:g0 + 512], ob[:, :], reads=[obb], pwrites=[kT_b[jg]])

                  def ev_xr(jg, pss):
                      (ps, psb), = pss
                      of, ofb = of_rot.next()
                      evac_copy(of[:, :], ps[:, :], [psb], writes=[ofb])
                      K.dma(QS, xrT_d[jg, :, g0:g0 + 512], of[:, :], reads=[ofb], pwrites=[xrT_b[jg]])

                  def ev_gr(jg, pss):
                      (ps, psb), = pss
                      of, ofb = of_rot.next()
                      evac_copy(of[:, :], ps[:, :], [psb], writes=[ofb])
                      K.dma(QS, grT_d[jg, :, t0:t0 + 512], of[:, :], reads=[ofb], pwrites=[grT_b[jg]])

                  def ev_v(tt, o, ps, psb):
                      ob, obb = ob_rot.next()
                      evac_copy(ob[:, :], ps[:, :], [psb], writes=[obb])
                      blk = (g0 // 128) + tt
                      nh = 512 // 128
                      dst = vS_d[o * nh:(o + 1) * nh, :, blk, :].rearrange("h p d -> p h d")
                      K.dma(QS, dst, ob[:, :].rearrange("p (h d) -> p h d", d=128), reads=[obb],
                            pwrites=[vS_b[o * nh + i] for i in range(nh)])
                  return ev_q, ev_k, ev_xr, ev_gr, ev_v

              def norm_job(sa):
                  def f(ents):
                      uT, uT_b = uTs[sa]
                      for tt in range(4):
                          norm_transpose(st, x_d[sa * 512 + tt * 128: sa * 512 + (tt + 1) * 128, :], [], "gmix", uT, uT_b, tt,
                                         rows_rot, xn_rot, sm_rot)
                  return ([], f)

              jobs = [norm_job(0)]
              for sa in range(cfg.NSTA):
                  g0 = sa * 512
                  is_ctx = g0 < C
                  t0 = g0 - C
                  uT, uT_b = uTs[sa]
                  ev_q, ev_k, ev_xr, ev_gr, ev_v = mk_evs(g0, t0)
                  jobs += gemm_f(wb_in, wbufs["in_k"], KC, DA, DA, uT, uT_b, ev_k)
                  if sa + 1 < cfg.NSTA:
                      jobs.append(norm_job(sa + 1))
                  jobs += gemm_t(wb_in, wbufs["in_v"], 0, KC, 2 * DA, DA, uT, uT_b, 0, ev_v)
                  jobs += gemm_f(wb_in, wbufs["in_xr"], KC, 3 * DA, DR, uT, uT_b, ev_xr)
                  if not is_ctx:
                      jobs += gemm_f(wb_in, wbufs["in_q"], KC, 0, DA, uT, uT_b, ev_q)
                      jobs += gemm_f(wb_in, wbufs["in_gr"], KC, 3 * DA + DR, DR, uT, uT_b, ev_gr)
              stream(ring, jobs)

        scale = 128.0 ** -0.5
        with (Stage("B") if "B" in cfg.stages else contextlib.nullcontext()) as st:
          if st is not None:
              assert H % 2 == 0
              kv_rot = Rot([(st.tile("kT%d" % i, [128, NTOK], BF16), st.tile("vh%d" % i, [128, NBLK, 128], BF16),
                             st.tile("qT%d" % i, [128, T], BF16)) for i in range(4)])
              e_rot = st.rot("e", [128, 1024], F32, 3)
              sp_rot = st.rot("sp", [128, 1024], BF16, 3)
              ec_rot = st.rot("ec", [128, 1024], F32, 2)
              w_rot = st.rot("w", [128, 1024], BF16, 3)
              R_rot = st.rot("R", [128, 1024], BF16, 2)
              sq_rot = st.rot("sq", [128, 512], BF16, 2)
              mo_rot = st.rot("mo", [128, 512], BF16, 2)
              S2, S2b = psBig[0], [psF[0][1], psF[1][1]]
              C2, C2b = psBig[1], [psF[2][1], psF[3][1]]
              Ob = [psF[4], psF[5]]
              pss_t, pss_b = psT[0]
              pss32 = pss_t[:, :].bitcast(F32)

              iters = []
              for hp in range(H // 2):
                  for qt in range(T // 512):
                      nkb = (C + (qt + 1) * 512) // 128
                      for it, kb in enumerate(reversed(range(nkb))):
                          iters.append(dict(hp=hp, qt=qt, it=it, kb=kb, n=nkb, r=kb - (C // 128 + 4 * qt)))
              NI = len(iters)
              hd = {}

              def load_head(h):
                  (kt, ktb), (vh, vhb), (qh, qhb) = kv_rot.next()
                  K.dma(QL, kt[:, :], kT_d[h, :, :], reads=[kT_b[h]], writes=[ktb])
                  K.dma(QL, vh[:, :, :], vS_d[h, :, :, :], reads=[vS_b[h]], writes=[vhb])
                  K.dma(QL, qh[:, :], qT_d[h, :, :], reads=[qT_b[h]], writes=[qhb])
                  hd[h] = (kt, ktb, vh, vhb, qh, qhb)

              state = {}
              for i in range(NI + 2):
                  if i % 3 == 0:
                      cast_tick(1)
                  if i < NI:
                      d = iters[i]
                      hp, qt = d["hp"], d["qt"]
                      if d["it"] == 0 and qt == 0 and hp == 0:
                          load_head(0)
                          load_head(1)
                      if d["it"] == 3 and qt == 0 and hp + 1 < H // 2:
                          load_head(2 * hp + 2)
                          load_head(2 * hp + 3)
                      for hh in range(2):
                          kt, ktb, vh, vhb, qh, qhb = hd[2 * hp + hh]
                          K.op(PE, lambda be, kt=kt, qh=qh, d=d, hh=hh: be.matmul(
                              S2[:, hh * 512:(hh + 1) * 512], lhsT=kt[:, d["kb"] * 128:(d["kb"] + 1) * 128],
                              rhs=qh[:, d["qt"] * 512:(d["qt"] + 1) * 512], start=True, stop=True),
                              reads=[ktb, qhb], writes=[S2b[hh]])
                      e, eb = e_rot.next()
                      K.op(ACT, lambda be, e=e: be.activation(out=e[:, :], in_=S2[:, :], func=AF.Exp, scale=scale),
                           reads=S2b, writes=[eb])
                      if d["r"] >= 0:
                          K.op(DVE, lambda be, e=e, r=d["r"]: be.tensor_tensor(
                              out=e[:, :].rearrange("p (h t) -> p h t", h=2), in0=e[:, :].rearrange("p (h t) -> p h t", h=2),
                              in1=maskM(r).unsqueeze(1).to_broadcast([128, 2, 512]), op=ALU.mult),
                              reads=[eb, cst32_b], writes=[eb])
                      sp, spb = sp_rot.next()
                      K.op(ACT, lambda be, e=e, sp=sp: be.activation(out=sp[:, :], in_=e[:, :], func=AF.Ln, bias=1.0),
                           reads=[eb], writes=[spb])
                      Rn, Rnb = R_rot.next()
                      if d["it"] == 0:
                          K.op(DVE, lambda be, Rn=Rn: be.memset(Rn[:, :], 0.0), writes=[Rnb])
                      else:
                          pst = state[i - 1]
                          K.op(DVE, lambda be, Rn=Rn, pst=pst: be.tensor_tensor(out=Rn[:, :], in0=pst["R"][:, :], in1=pst["sp"][:, :],
                                                                          op=ALU.add),
                               reads=[pst["Rb"], pst["spb"]], writes=[Rnb])
                      state[i] = dict(e=e, eb=eb, sp=sp, spb=spb, R=Rn, Rb=Rnb, d=d)
                  if 0 <= i - 1 < NI:
                      s1 = state[i - 1]
                      for hh in range(2):
                          cs = slice(hh * 512, (hh + 1) * 512)
                          K.op(PE, lambda be, s1=s1, cs=cs: be.matmul(C2[:, cs], lhsT=Lmat, rhs=s1["sp"][:, cs], start=True, stop=False),
                               reads=[cbf_b, s1["spb"]], writes=[C2b[hh]], mark=False)
                          K.op(PE, lambda be, s1=s1, cs=cs: be.matmul(C2[:, cs], lhsT=ones, rhs=s1["R"][:, cs], start=False, stop=True),
                               reads=[cbf_b, s1["Rb"]], writes=[C2b[hh]])
                      ec, ecb = ec_rot.next()
                      K.op(ACT, lambda be, ec=ec: be.activation(out=ec[:, :], in_=C2[:, :], func=AF.Exp, scale=-1.0),
                           reads=C2b, writes=[ecb])
                      w, wb_ = w_rot.next()
                      K.op(DVE, lambda be, w=w, s1=s1, ec=ec: be.tensor_tensor(out=w[:, :], in0=s1["e"][:, :], in1=ec[:, :], op=ALU.mult),
                           reads=[s1["eb"], ecb], writes=[wb_])
                      s1["w"], s1["wb"] = w, wb_
                  if 0 <= i - 2 < NI:
                      s2 = state.pop(i - 2)
                      d2 = s2["d"]
                      last = d2["it"] == d2["n"] - 1
                      for hh in range(2):
                          h2 = 2 * d2["hp"] + hh
                          kt, ktb, vh, vhb, qh, qhb = hd[h2]
                          po, pob = Ob[hh]
                          cs = slice(hh * 512, (hh + 1) * 512)
                          K.op(PE, lambda be, po=po, vh=vh, s2=s2, d2=d2, last=last, cs=cs: be.matmul(
                              po[:, :], lhsT=vh[:, d2["kb"], :], rhs=s2["w"][:, cs], start=(d2["it"] == 0), stop=last),
                              reads=[vhb, s2["wb"]], writes=[pob], mark=True)
                          if last:
                              qt2 = d2["qt"]
                              sq, sqb = sq_rot.next()
                              mo, mob = mo_rot.next()
                              go, _ = cfg.pv["gA"]
                              K.op(DVE, lambda be, mo=mo, po=po, h2=h2, go=go: be.tensor_scalar(
                                  out=mo[:, :], in0=po[:, :], scalar1=pvt[:, go + h2:go + h2 + 1], scalar2=None, op0=ALU.mult),
                                  reads=[pob, pvt_b], writes=[mob])
                              K.op(ACT, lambda be, sq=sq, po=po: be.activation(out=sq[:, :], in_=po[:, :], func=AF.Square),
                                   reads=[pob, mob], writes=[sqb])
                              K.dma(QS, mixT_d[h2, :, qt2 * 512:(qt2 + 1) * 512], mo[:, :], reads=[mob], pwrites=[mixT_b[h2]])
                              for tq in range(4):
                                  K.op(PE, lambda be, sq=sq, tq=tq: be.matmul(pss32[:, tq:tq + 1], lhsT=sq[:, tq * 128:(tq + 1) * 128],
                                                                           rhs=ones[:, 0:1], start=True, stop=True),
                                       reads=[sqb, cbf_b], writes=[pss_b], mark=(tq == 3))
                              K.op(DVE, lambda be, qt2=qt2: be.tensor_tensor(out=ssA[:, qt2 * 4:qt2 * 4 + 4], in0=ssA[:, qt2 * 4:qt2 * 4 + 4],
                                                                          in1=pss32[:, 0:4], op=ALU.add),
                                   reads=[pss_b, ss_b], writes=[ss_b])

        cast_flush_buf(wbufs["ra"])
        cast_flush_buf(wbufs["rx"])
        with (Stage("R") if "R" in cfg.stages else contextlib.nullcontext()) as st:
          if st is not None:
              L = max(C, T)
              assert C % 512 == 0 and T % 512 == 0
              set_rot = Rot([dict(XR=st.tile("XR%d" % i, [128, L + 4], F32), Y=st.tile("Y%d" % i, [128, L], F32),
                                  YB=st.tile("YB%d" % i, [128, L], BF16), Rb=st.tile("Rb%d" % i, [128, L], F32),
                                  Ab=st.tile("Ab%d" % i, [128, L], F32), Ib=st.tile("Ib%d" % i, [128, L], F32)) for i in range(3)])
              GR_rot = st.rot("GR", [128, T], F32, 3)
              GT, GT_b = st.tile("GT", [128, T], F32)
              Hs, Hs_b = st.tile("Hs", [128, T], F32)
              SQ_rot = st.rot("SQ", [128, T], BF16, 2)
              MO_rot = st.rot("MO", [128, T], BF16, 2)
              Wa, Wa_b = st.tile("Wa", [128, NB, 128], BF16)
              Wx, Wx_b = st.tile("Wx", [128, NB, 128], BF16)
              h0_rot = st.rot("h0", [128, 2], F32, 3, small=True)
              pss_t, pss_b = psT[0]
              pss32 = pss_t[:, :].bitcast(F32)
              K.dma(QL, Wa[:, :, :], wb_ra.rearrange("(n i) j -> i n j", i=128), reads=[wbufs["ra"]], writes=[Wa_b])
              K.dma(QL, Wx[:, :, :], wb_rx.rearrange("(n i) j -> i n j", i=128), reads=[wbufs["rx"]], writes=[Wx_b])
              cwo, _ = cfg.pv["cw"]
              cbo, _ = cfg.pv["cb"]
              bao, _ = cfg.pv["ba"]
              bxo, _ = cfg.pv["bx"]
              gRo, _ = cfg.pv["gR"]
              units = []
              for c in range(NB):
                  if C > 0:
                      units.append((c, 0, 0, C))
                  units.append((c, 1, C, T))
              ust = {}
              h0s = {}

              def front_a1(u):
                  c, sgi, s0, n = units[u]
                  S_ = set_rot.next()
                  (XR, XR_b), (Y, Y_b), (YB, YB_b) = S_["XR"], S_["Y"], S_["YB"]
                  (Rb_, Rb_b), (Ab_, Ab_b), (Ib_, Ib_b) = S_["Rb"], S_["Ab"], S_["Ib"]
                  st_ = dict(S_)
                  ust[u] = st_
                  if s0 == 0:
                      K.dma(QL, XR[:, 4:4 + n], xrT_d[c, :, 0:n], reads=[xrT_b[c]], writes=[XR_b])
                      K.op(DVE, lambda be: be.memset(XR[:, 0:4], 0.0), pwrites=[XR_b])
                  else:
                      K.dma(QL, XR[:, 0:4 + n], xrT_d[c, :, s0 - 4:s0 + n], reads=[xrT_b[c]], writes=[XR_b])
                  if sgi == 1:
                      GR, GR_b = GR_rot.next()
                      st_["GR"] = (GR, GR_b)
                      K.dma(QL, GR[:, :], grT_d[c, :, :], reads=[grT_b[c]], writes=[GR_b])
                      G_ = GT[:, 0:T]
                      K.op(ACT, lambda be: be.activation(out=G_, in_=GR[:, :], func=AF.Square), reads=[GR_b], writes=[GT_b])
                      K.op(ACT, lambda be: be.activation(out=G_, in_=G_, func=AF.Identity, scale=0.044715, bias=1.0),
                           reads=[GT_b], writes=[GT_b])
                      K.op(POOL, lambda be: be.tensor_tensor(out=G_, in0=G_, in1=GR[:, :], op=ALU.mult), reads=[GT_b, GR_b], writes=[GT_b])
                  K.op(ACT, lambda be: be.activation(out=Y[:, 0:n], in_=XR[:, 4:4 + n], func=AF.Identity,
                                                     scale=pvt[:, cwo + c * 4 + 3:cwo + c * 4 + 4], bias=pvt[:, cbo + c:cbo + c + 1]),
                       reads=[XR_b, pvt_b], writes=[Y_b])
                  for i_ in range(3):
                      sh = 3 - i_
                      K.op(DVE, lambda be, i_=i_, sh=sh: be.scalar_tensor_tensor(
                          out=Y[:, 0:n], in0=XR[:, 4 - sh:4 - sh + n], scalar=pvt[:, cwo + c * 4 + i_:cwo + c * 4 + i_ + 1],
                          in1=Y[:, 0:n], op0=ALU.mult, op1=ALU.add), reads=[XR_b, pvt_b, Y_b], writes=[Y_b])

              def front_a2(u):
                  c, sgi, s0, n = units[u]
                  st_ = ust[u]
                  (Y, Y_b), (YB, YB_b) = st_["Y"], st_["YB"]
                  K.op(ACT, lambda be: be.activation(out=YB[:, 0:n], in_=Y[:, 0:n], func=AF.Copy), reads=[Y_b], writes=[YB_b])

              def front_b(u):
                  c, sgi, s0, n = units[u]
                  st_ = ust[u]
                  (XR, XR_b), (Y, Y_b), (YB, YB_b) = st_["XR"], st_["Y"], st_["YB"]
                  (Rb_, Rb_b), (Ab_, Ab_b), (Ib_, Ib_b) = st_["Rb"], st_["Ab"], st_["Ib"]
                  if sgi == 1:
                      GR, GR_b = st_["GR"]
                      G_ = GT[:, 0:T]
                      K.op(ACT, lambda be: be.activation(out=G_, in_=G_, func=AF.Sigmoid, scale=1.5957691216057308),
                           reads=[GT_b], writes=[GT_b])
                      K.op(POOL, lambda be: be.tensor_tensor(out=GR[:, :], in0=G_, in1=GR[:, :], op=ALU.mult), reads=[GT_b, GR_b], writes=[GR_b])
                  for tq in range(n // 512):
                      pa, pab = psF_rot.next()
                      K.op(PE, lambda be, pa=pa, tq=tq: be.matmul(pa[:, :], lhsT=Wa[:, c, :], rhs=YB[:, tq * 512:(tq + 1) * 512],
                                                                  start=True, stop=True), reads=[Wa_b, YB_b], writes=[pab])
                      K.op(ACT, lambda be, pa=pa, tq=tq: be.activation(out=Rb_[:, tq * 512:(tq + 1) * 512], in_=pa[:, :], func=AF.Sigmoid,
                                                                       bias=pvt[:, bao + c:bao + c + 1]),
                           reads=[pab, pvt_b], pwrites=[Rb_b])
                      px, pxb = psF_rot.next()
                      K.op(PE, lambda be, px=px, tq=tq: be.matmul(px[:, :], lhsT=Wx[:, c, :], rhs=YB[:, tq * 512:(tq + 1) * 512],
                                                                  start=True, stop=True), reads=[Wx_b, YB_b], writes=[pxb])
                      K.op(ACT, lambda be, px=px, tq=tq: be.activation(out=Ib_[:, tq * 512:(tq + 1) * 512], in_=px[:, :], func=AF.Sigmoid,
                                                                       bias=pvt[:, bxo + c:bxo + c + 1]),
                           reads=[pxb, pvt_b], pwrites=[Ib_b])
                  Mb_ = XR[:, 4:4 + n]
                  K.op(ACT, lambda be: be.activation(out=Ab_[:, 0:n], in_=Rb_[:, 0:n], func=AF.Exp, scale=coef[:, c:c + 1]),
                       reads=[Rb_b, coef_b], writes=[Ab_b])
                  K.op(ACT, lambda be: be.activation(out=Mb_, in_=Rb_[:, 0:n], func=AF.Exp, scale=coef[:, NB + c:NB + c + 1]),
                       reads=[Rb_b, coef_b, Y_b], writes=[XR_b])
                  K.op(ACT, lambda be: be.activation(out=Mb_, in_=Mb_, func=AF.Sqrt, scale=-1.0, bias=1.0),
                       reads=[XR_b], writes=[XR_b])
                  K.op(POOL, lambda be: be.tensor_tensor(out=Ib_[:, 0:n], in0=Ib_[:, 0:n], in1=Y[:, 0:n], op=ALU.mult),
                       reads=[Ib_b, Y_b], writes=[Ib_b])
                  K.op(POOL, lambda be: be.tensor_tensor(out=Ib_[:, 0:n], in0=Ib_[:, 0:n], in1=Mb_, op=ALU.mult),
                       reads=[Ib_b, XR_b], writes=[Ib_b])

              def back(u):
                  c, sgi, s0, n = units[u]
                  st_ = ust.pop(u)
                  (Rb_, Rb_b), (Ab_, Ab_b), (Ib_, Ib_b) = st_["Rb"], st_["Ab"], st_["Ib"]
                  if sgi == 0:
                      K.op(DVE, lambda be: be.tensor_tensor_scan(out=Rb_[:, 0:n], data0=Ab_[:, 0:n], data1=Ib_[:, 0:n], initial=0.0,
                                                                 op0=ALU.mult, op1=ALU.add),
                           reads=[Ab_b, Ib_b], writes=[Rb_b])
                      h0, h0_b = h0_rot.next()
                      h0s[c] = (h0, h0_b)
                      K.op(DVE, lambda be: be.tensor_tensor(out=h0[:, 0:1], in0=Rb_[:, n - 1:n], in1=flag_t[:, 0:1], op=ALU.mult),
                           reads=[Rb_b, flag_b], writes=[h0_b], force=True)
                      return
                  GR, GR_b = st_["GR"]
                  if c in h0s:
                      h0, h0_b = h0s.pop(c)
                      K.op(DVE, lambda be: be.tensor_tensor_scan(out=Hs[:, 0:n], data0=Ab_[:, 0:n], data1=Ib_[:, 0:n], initial=h0[:, 0:1],
                                                                 op0=ALU.mult, op1=ALU.add),
                           reads=[Ab_b, Ib_b, h0_b], writes=[Hs_b])
                  else:
                      K.op(DVE, lambda be: be.tensor_tensor_scan(out=Hs[:, 0:n], data0=Ab_[:, 0:n], data1=Ib_[:, 0:n], initial=0.0,
                                                                 op0=ALU.mult, op1=ALU.add),
                           reads=[Ab_b, Ib_b], writes=[Hs_b])
                  SQ, SQ_b = SQ_rot.next()
                  MO, MO_b = MO_rot.next()
                  K.op(DVE, lambda be: be.tensor_tensor(out=Hs[:, :], in0=GR[:, :], in1=Hs[:, :], op=ALU.mult),
                       reads=[GR_b, Hs_b], writes=[Hs_b])
                  K.op(ACT, lambda be: be.activation(out=SQ[:, :], in_=Hs[:, :], func=AF.Square), reads=[Hs_b], writes=[SQ_b])
                  K.op(DVE, lambda be: be.tensor_scalar(out=MO[:, :], in0=Hs[:, :], scalar1=pvt[:, gRo + c:gRo + c + 1], scalar2=None,
                                                        op0=ALU.mult), reads=[Hs_b, pvt_b], writes=[MO_b])
                  K.dma(QS, mixT_d[H + c, :, :], MO[:, :], reads=[MO_b], writes=[mixT_b[H + c]])
                  NTQ = T // 128
                  for tq in range(NTQ):
                      K.op(PE, lambda be, tq=tq: be.matmul(pss32[:, tq:tq + 1], lhsT=SQ[:, tq * 128:(tq + 1) * 128], rhs=ones[:, 0:1],
                                                           start=True, stop=True), reads=[SQ_b, cbf_b], writes=[pss_b], mark=(tq == NTQ - 1))
                  K.op(DVE, lambda be: be.tensor_tensor(out=ssR[:, :], in0=ssR[:, :], in1=pss32[:, 0:NTQ], op=ALU.add),
                       reads=[pss_b, ss_b], writes=[ss_b])

              NU = len(units)
              front_a1(0)
              front_a2(0)
              front_b(0)
              if NU > 1:
                  front_a1(1)
                  front_a2(1)
              for u in range(NU):
                  cast_tick(4, paced=False)
                  if u + 2 < NU:
                      front_a1(u + 2)
                  if u + 1 < NU:
                      front_b(u + 1)
                  if u + 2 < NU:
                      front_a2(u + 2)
                  back(u)
              K.op(DVE, lambda be: be.tensor_scalar(out=ssA[:, :], in0=ssA[:, :], scalar1=1.0 / DA, scalar2=EPS, op0=ALU.mult, op1=ALU.add),
                   reads=[ss_b], writes=[ss_b])
              K.op(ACT, lambda be: be.activation(out=ssA[:, :], in_=ssA[:, :], func=AF.Ln), reads=[ss_b], writes=[ss_b])
              K.op(ACT, lambda be: be.activation(out=ssA[:, :], in_=ssA[:, :], func=AF.Exp, scale=-0.5), reads=[ss_b], writes=[ss_b])
              K.op(DVE, lambda be: be.tensor_scalar(out=ssR[:, :], in0=ssR[:, :], scalar1=1.0 / DR, scalar2=EPS, op0=ALU.mult, op1=ALU.add),
                   reads=[ss_b], writes=[ss_b])
              K.op(ACT, lambda be: be.activation(out=ssR[:, :], in_=ssR[:, :], func=AF.Ln), reads=[ss_b], writes=[ss_b])
              K.op(ACT, lambda be: be.activation(out=ssR[:, :], in_=ssR[:, :], func=AF.Exp, scale=-0.5), reads=[ss_b], writes=[ss_b])
              if cfg.debug:
                  K.dma(QS, ss_d[:, 0:T // 128], ssA[:, :], reads=[ss_b])
                  K.dma(QS, ss_d[:, T // 128:], ssR[:, :], reads=[ss_b])

        cast_flush()
        with (Stage("C") if "C" in cfg.stages else contextlib.nullcontext()) as st:
          if st is not None:
              ring = make_ring(st)
              rows_rot = st.rot("row", [128, D], F32, 1)
              xn_rot = st.rot("xn", [128, D], BF16, 1)
              sm_rot = st.rot("sm", [128, 4], F32, 4, small=True)
              actT, actT_b = st.tile("actT", [128, KC, 512], BF16)
              FG = cfg.NFFG
              gsz = [(FFC + FG - 1 - g) // FG for g in range(FG)]
              aTf, aT_b = st.tile("aT", [128, max(gsz) * 512], BF16)
              aT = aTf[:, :].rearrange("p (g t) -> p g t", t=512)
              gfin = aTf[:, 0:2 * D].bitcast(F32)
              gfin_b = aT_b
              pc_rot = st.rot("pc", [128, 512], F32, 8)
              sg_rot = st.rot("sg", [128, 512], F32, 3)
              gp_rot = st.rot("gp", [128, 512], F32, 3)
              Wpp, Wpp_b = st.tile("Wpp", [128, PC, D], BF16)
              pT, pT_b = st.tile("pT", [128, PC, 512], BF16)
              prow_rot = st.rot("prow", [128, PLE], F32, 2)
              pbf_rot = st.rot("pbf", [128, PLE], BF16, 2)
              sspe, sspe_b = st.tile("sspe", [128, 8], F32, small=True)
              K.dma(QL, Wpp[:, :, :], wb_pp.rearrange("(k p) n -> p k n", p=128), reads=[wbufs["pp"]], writes=[Wpp_b])

              NO = D // 512
              pe32 = psT[1][0][:, :].bitcast(F32)

              def st_jobs(s_, prev_final=None):
                  t0 = s_ * 512
                  jobs = []
                  pre_pc = {}

                  def load_pc(o, first):
                      for tt in range(4):
                          pc, pcb = pc_rot.next()
                          r0 = t0 + tt * 128
                          if first:
                              K.dma(QL, pc[:, :], x_d[C + r0:C + r0 + 128, o * 512:(o + 1) * 512], writes=[pcb])
                          else:
                              K.dma(QL, pc[:, :], hS_d[r0:r0 + 128, o * 512:(o + 1) * 512], reads=[hb(s_, tt, o)], writes=[pcb])
                          pre_pc[(tt, o)] = (pc, pcb)

                  def mk_pre(first):
                      def pre(o):
                          if o == 0:
                              load_pc(0, first)
                          if o + 1 < NO:
                              load_pc(o + 1, first)
                      return pre

                  def mk_ev_res(ssrc):
                      def ev(tt, o, ps, psb):
                          pc, pcb = pre_pc.pop((tt, o))
                          r0 = t0 + tt * 128
                          idx = s_ * 4 + tt
                          if ssrc is None:
                              K.op(DVE, lambda be: be.tensor_tensor(out=pc[:, :], in0=ps[:, :], in1=pc[:, :], op=ALU.add),
                                   reads=[psb, pcb], writes=[pcb])
                          else:
                              K.op(DVE, lambda be: be.scalar_tensor_tensor(out=pc[:, :], in0=ps[:, :], scalar=ssrc[:, idx:idx + 1],
                                                                           in1=pc[:, :], op0=ALU.mult, op1=ALU.add),
                                   reads=[psb, pcb, ss_b], writes=[pcb])
                          K.dma(QS, hS_d[r0:r0 + 128, o * 512:(o + 1) * 512], pc[:, :], reads=[pcb], writes=[hb(s_, tt, o)])
                      return ev

                  def load_mix(ents, t0=t0):
                      K.dma(QL, actT[:, :, :], mixT_d[:, :, t0:t0 + 512].rearrange("c p t -> p c t"), reads=mixT_b, writes=[actT_b])
                  if s_ == 0:
                      jobs.append(([], load_mix))
                  jobs += gemm_t(wb_out, wbufs["out"], 0, H, 0, D, actT, actT_b, 0, mk_ev_res(ssA), pre=mk_pre(True))
                  jobs += gemm_t(wb_out, wbufs["out"], DA, NB, 0, D, actT, actT_b, H, mk_ev_res(ssR), pre=mk_pre(False))

                  def ffn_norm(ents):
                      for tt in range(4):
                          r0 = t0 + tt * 128
                          norm_transpose(st, hS_d[r0:r0 + 128, :], [hb(s_, tt, o) for o in range(NO)], "gffn", actT, actT_b, tt,
                                         rows_rot, xn_rot, sm_rot)
                  jobs.append(([], ffn_norm))
                  ffn_pos = [len(jobs) + 6]

                  def ev_gu(jg, pss):
                      (pg, pgb), (pu, pub) = pss
                      sg, sgb = sg_rot.next()
                      K.op(ACT, lambda be: be.activation(out=sg[:, :], in_=pg[:, :], func=AF.Silu), reads=[pgb], writes=[sgb])
                      K.op(DVE, lambda be: be.tensor_tensor(out=aT[:, jg, :], in0=sg[:, :], in1=pu[:, :], op=ALU.mult),
                           reads=[sgb, pub], pwrites=[aT_b])
                  f0 = 0
                  for g in range(FG):
                      ng = gsz[g]
                      jobs += gemm_f(wb_gate, wbufs["gate"], KC, f0 * 128, ng * 128, actT, actT_b, ev_gu, pair=(wb_up, wbufs["up"]))
                      jobs += gemm_t(wb_down, wbufs["down"], f0 * 128, ng, 0, D, aT, aT_b, 0, mk_ev_res(None), pre=mk_pre(False))
                      f0 += ng

                  def ple_norm(ents):
                      for tt in range(4):
                          r0 = t0 + tt * 128
                          norm_transpose(st, hS_d[r0:r0 + 128, :], [hb(s_, tt, o) for o in range(NO)], "gple", actT, actT_b, tt,
                                         rows_rot, xn_rot, sm_rot)

                  def ple_prep(ents):
                      for tt in range(4):
                          r0 = t0 + tt * 128
                          prow, prow_b = prow_rot.next()
                          pbf, pbf_b = pbf_rot.next()
                          K.dma(QL, prow[:, :], p_d[r0:r0 + 128, :], writes=[prow_b])
                          K.op(ACT, lambda be, pbf=pbf, prow=prow: be.activation(out=pbf[:, :], in_=prow[:, :], func=AF.Copy),
                               reads=[prow_b], writes=[pbf_b])
                          pt, ptb = psT_rot.next()
                          for j in range(PC):
                              K.op(PE, lambda be, pt=pt, j=j, pbf=pbf: be.transpose(out=pt[:, j * 128:(j + 1) * 128],
                                                                                  in_=pbf[:, j * 128:(j + 1) * 128], identity=ident),
                                   reads=[pbf_b, cbf_b], writes=[ptb], mark=(j == PC - 1))
                          K.op(DVE, lambda be, pt=pt, tt=tt: be.tensor_copy(out=pT[:, :, tt * 128:(tt + 1) * 128],
                                                                            in_=pt[:, 0:PC * 128].rearrange("p (g t) -> p g t", t=128)),
                               reads=[ptb], pwrites=[pT_b])
                      for tt in range(4):
                          for o in range(NO):
                              ps, psb = psF_rot.next()
                              for kc in range(PC):
                                  K.op(PE, lambda be, ps=ps, kc=kc, tt=tt, o=o: be.matmul(
                                      ps[:, :], lhsT=pT[:, kc, tt * 128:(tt + 1) * 128], rhs=Wpp[:, kc, o * 512:(o + 1) * 512],
                                      start=(kc == 0), stop=(kc == PC - 1)), reads=[pT_b, Wpp_b], writes=[psb], mark=(kc == PC - 1))
                              sg, sgb = sg_rot.next()
                              sm, sm_b = sm_rot.next()
                              K.op(ACT, lambda be, sg=sg, ps=ps, sm=sm: be.activation(out=sg[:, :], in_=ps[:, :], func=AF.Square,
                                                                                      accum_out=sm[:, 0:1]),
                                   reads=[psb], writes=[sgb, sm_b])
                              if o == 0:
                                  K.op(DVE, lambda be, sm=sm, tt=tt: be.tensor_copy(out=sspe[:, tt:tt + 1], in_=sm[:, 0:1]),
                                       reads=[sm_b], writes=[sspe_b])
                              else:
                                  K.op(DVE, lambda be, sm=sm, tt=tt: be.tensor_tensor(out=sspe[:, tt:tt + 1], in0=sspe[:, tt:tt + 1],
                                                                                    in1=sm[:, 0:1], op=ALU.add),
                                       reads=[sm_b, sspe_b], writes=[sspe_b])
                      K.op(DVE, lambda be: be.tensor_scalar(out=sspe[:, 4:8], in0=sspe[:, 0:4], scalar1=1.0 / D, scalar2=EPS,
                                                            op0=ALU.mult, op1=ALU.add), reads=[sspe_b], writes=[sspe_b])
                      K.op(ACT, lambda be: be.activation(out=sspe[:, 4:8], in_=sspe[:, 4:8], func=AF.Ln), reads=[sspe_b], writes=[sspe_b])
                      K.op(ACT, lambda be: be.activation(out=sspe[:, 4:8], in_=sspe[:, 4:8], func=AF.Exp, scale=-0.5),
                           reads=[sspe_b], writes=[sspe_b])
                  jobs.insert(ffn_pos[0], ([], ple_prep))
                  jobs.append(([], ple_norm))
                  gp_d = {}

                  def pre_ple(o):
                      def ld(o2):
                          gp, gpb = gp_rot.next()
                          K.dma(QL, gp[:, :], gpo_d[0:1, o2 * 512:(o2 + 1) * 512].partition_broadcast(128), writes=[gpb])
                          gp_d[o2] = (gp, gpb)
                          load_pc(o2, False)
                      if o == 0:
                          ld(0)
                      if o + 1 < NO:
                          ld(o + 1)

                  def ev_ple(tt, o, ps, psb):
                      r0 = t0 + tt * 128
                      gp, gpb = gp_d[o]
                      sg, sgb = sg_rot.next()
                      K.op(ACT, lambda be: be.activation(out=sg[:, :], in_=ps[:, :], func=AF.Sigmoid), reads=[psb], writes=[sgb])
                      pe_, pe_b = pe32, psT[1][1]
                      for kc in range(PC):
                          K.op(PE, lambda be, kc=kc: be.matmul(pe_[:, :], lhsT=pT[:, kc, tt * 128:(tt + 1) * 128],
                                                               rhs=Wpp[:, kc, o * 512:(o + 1) * 512], start=(kc == 0), stop=(kc == PC - 1)),
                               reads=[pT_b, Wpp_b], writes=[pe_b], mark=(kc == PC - 1))
                      sg2, sg2b = sg_rot.next()
                      K.op(DVE, lambda be: be.scalar_tensor_tensor(out=sg2[:, :], in0=pe_[:, :], scalar=sspe[:, 4 + tt:5 + tt], in1=gp[:, :],
                                                                   op0=ALU.mult, op1=ALU.mult), reads=[pe_b, sspe_b, gpb], writes=[sg2b])
                      K.op(DVE, lambda be: be.tensor_tensor(out=sg2[:, :], in0=sg2[:, :], in1=sg[:, :], op=ALU.mult),
                           reads=[sgb, sg2b], writes=[sg2b])
                      pc, pcb = pre_pc.pop((tt, o))
                      K.op(DVE, lambda be: be.tensor_tensor(out=pc[:, :], in0=pc[:, :], in1=sg2[:, :], op=ALU.add),
                           reads=[sg2b, pcb], writes=[pcb])
                      K.dma(QS, hS_d[r0:r0 + 128, o * 512:(o + 1) * 512], pc[:, :], reads=[pcb], writes=[hb(s_, tt, o)])
                  jobs += gemm_t(wb_pg, wbufs["pg"], 0, KC, 0, D, actT, actT_b, 0, ev_ple, pre=pre_ple)

                  def final_norm(ents):
                      for tt in range(4):
                          r0 = t0 + tt * 128
                          hrow, hrow_b = rows_rot.next()
                          xn, xn_b = xn_rot.next()
                          sm, sm_b = sm_rot.next()
                          K.dma(QL, hrow[:, :], hS_d[r0:r0 + 128, :], reads=[hb(s_, tt, o) for o in range(NO)], writes=[hrow_b])
                          K.op(ACT, lambda be, xn=xn, hrow=hrow, sm=sm: be.activation(out=xn[:, :], in_=hrow[:, :], func=AF.Square,
                                                                                      accum_out=sm[:, 0:1]),
                               reads=[hrow_b], writes=[xn_b, sm_b])
                          K.op(DVE, lambda be, sm=sm: be.tensor_scalar(out=sm[:, 1:2], in0=sm[:, 0:1], scalar1=1.0 / D, scalar2=EPS,
                                                                       op0=ALU.mult, op1=ALU.add), reads=[sm_b], writes=[sm_b])
                          K.op(ACT, lambda be, sm=sm: be.activation(out=sm[:, 2:3], in_=sm[:, 1:2], func=AF.Ln), reads=[sm_b], writes=[sm_b])
                          K.op(ACT, lambda be, sm=sm: be.activation(out=sm[:, 2:3], in_=sm[:, 2:3], func=AF.Exp, scale=-0.5),
                               reads=[sm_b], writes=[sm_b])
                          for o in range(NO):
                              gp, gpb = gp_rot.next()
                              K.dma(QL, gp[:, :], gfin_d[0:1, o * 512:(o + 1) * 512].partition_broadcast(128), writes=[gpb])
                              K.op(DVE, lambda be, hrow=hrow, sm=sm, gp=gp, o=o: be.scalar_tensor_tensor(
                                  out=hrow[:, o * 512:(o + 1) * 512], in0=hrow[:, o * 512:(o + 1) * 512], scalar=sm[:, 2:3],
                                  in1=gp[:, :], op0=ALU.mult, op1=ALU.mult),
                                  reads=[hrow_b, sm_b, gpb], writes=[hrow_b])
                          K.dma(QS, out_d[r0:r0 + 128, :], hrow[:, :], reads=[hrow_b])
                  if s_ + 1 < cfg.NST:
                      jobs.append(([], lambda ents: K.dma(QL, actT[:, :, :], mixT_d[:, :, t0 + 512:t0 + 1024].rearrange("c p t -> p c t"),
                                                          reads=mixT_b, writes=[actT_b])))
                  if prev_final is not None:
                      jobs.insert(ffn_pos[0] - 3, ([], prev_final))
                  if s_ + 1 == cfg.NST:
                      jobs.append(([], final_norm))
                  return jobs, final_norm

              alljobs = []
              pf = None
              for s_ in range(cfg.NST):
                  j_, pf = st_jobs(s_, pf)
                  alljobs += j_
              stream(ring, alljobs)

        K.finish()

        with nc.Block() as block:
            @block.tensor
            def _(be):
                for f in PE.prog:
                    f(be)

            @block.scalar
            def _(be):
                for f in ACT.prog:
                    f(be)

            @block.vector
            def _(be):
                for f in DVE.prog:
                    f(be)

            @block.gpsimd
            def _(be):
                for f in POOL.prog:
                    f(be)

            @block.sync
            def _(be):
                for f in SP.prog:
                    f(be)
        nc._k_stats = (K.nops, K.ndma)
    return nc


def make_consts(cfg):
    c = np.zeros((128, cfg.NCONST), np.float32)
    c[:, 0:128] = np.eye(128, dtype=np.float32)
    j = np.arange(128)[:, None]
    s = np.arange(128)[None, :]
    c[:, 128:256] = (j >= s).astype(np.float32)
    c[:, 256:384] = 1.0
    col = np.arange(512)[None, :]
    for r in range(4):
        c[:, 384 + r * 512:384 + (r + 1) * 512] = ((r * 128 + j) < col).astype(np.float32)
    return c


def colmajor(v, n):
    return np.ascontiguousarray(np.asarray(v, np.float32).reshape(n, 128).T)


def make_pv(cfg, g_mix, g_ffn, g_ple, conv_w, conv_b, b_rg_a, b_rg_x, rg_lambda, g_attn_out, g_rnn_out):
    pv = np.zeros((128, cfg.NPV), np.float32)

    def put(name, arr):
        o, n = cfg.pv[name]
        pv[:, o:o + n] = arr
    put("gmix", colmajor(g_mix, cfg.KC))
    put("gffn", colmajor(g_ffn, cfg.KC))
    put("gple", colmajor(g_ple, cfg.KC))
    cw = np.asarray(conv_w, np.float32)
    cwp = np.stack([colmajor(cw[i], cfg.NB) for i in range(4)], axis=2)
    put("cw", cwp.reshape(128, cfg.NB * 4))
    put("cb", colmajor(conv_b, cfg.NB))
    put("ba", colmajor(b_rg_a, cfg.NB))
    put("bx", colmajor(b_rg_x, cfg.NB))
    put("lam", colmajor(rg_lambda, cfg.NB))
    put("gA", colmajor(g_attn_out, cfg.H))
    put("gR", colmajor(g_rnn_out, cfg.NB))
    return pv


def make_in_maps(cfg, nbatch, x, p, g_mix, w_in, conv_w, conv_b, w_rg_a, b_rg_a, w_rg_x, b_rg_x, rg_lambda, g_attn_out,
                 g_rnn_out, w_out, g_ffn, w_ffn_gate, w_ffn_up, w_ffn_down, g_ple, w_ple_gate, w_ple_proj, g_ple_out,
                 g_final):
    f = lambda a: np.ascontiguousarray(np.asarray(a, np.float32))
    shared = {
        "pv": make_pv(cfg, g_mix[0], g_ffn[0], g_ple[0], conv_w[0], conv_b[0], b_rg_a[0], b_rg_x[0], rg_lambda[0],
                      g_attn_out[0], g_rnn_out[0]),
        "consts": make_consts(cfg),
        "gpo": f(g_ple_out[0]).reshape(1, cfg.D),
        "gfin": f(g_final).reshape(1, cfg.D),
        "w_in": f(w_in[0]), "w_out": f(w_out[0]), "w_gate": f(w_ffn_gate[0]), "w_up": f(w_ffn_up[0]),
        "w_down": f(w_ffn_down[0]), "w_pg": f(w_ple_gate[0]), "w_pp": f(w_ple_proj[0]),
        "w_ra": f(w_rg_a[0]).reshape(cfg.NB * 128, 128), "w_rx": f(w_rg_x[0]).reshape(cfg.NB * 128, 128),
    }
    x = np.asarray(x, np.float32)
    p = np.asarray(p, np.float32)
    in_maps = []
    for b in range(nbatch):
        for half in range(2):
            m = dict(shared)
            if half == 0:
                xc = np.concatenate([np.zeros((cfg.C, cfg.D), np.float32), x[b, 0:cfg.T]], axis=0)
            else:
                xc = np.ascontiguousarray(x[b, 0:cfg.C + cfg.T])
            m["x"] = xc
            m["p"] = np.ascontiguousarray(p[0, b, half * cfg.T:(half + 1) * cfg.T])
            m["flag"] = np.full((128, 1), float(half), np.float32)
            in_maps.append(m)
    return in_maps


def kernel(**inputs):
    cfg = Cfg()
    nc = build_nc(cfg)
    in_maps = make_in_maps(cfg, 4, **inputs)
    res = run_bass_kernel_spmd(nc, in_maps, core_ids=list(range(8)))
    out = np.empty((4, 4096, 4096), np.float32)
    for b in range(4):
        for half in range(2):
            out[b, half * 2048:(half + 1) * 2048] = res.results[b * 2 + half]["out"]
    return out
```
